# Optimizing a Trainium2 kernel written in Bass

```python
import math
import jax, jax.numpy as jnp
from jax import lax
import numpy as np

D_MODEL = 1024
BATCH = 4
SEQ = 8192
DEPTH = 4

N_MIXERS = 3
N_A = (DEPTH + 2) // 3
N_B = (DEPTH + 1) // 3
N_C = DEPTH // 3

HEAD_DIM = 64
N_HEADS = D_MODEL // HEAD_DIM
N_KV = 4
GROUP = N_HEADS // N_KV
WINDOW = 128
BLOCK = 128
QKV_DIM = (N_HEADS + 2 * N_KV) * HEAD_DIM

CONV_K = 31

CHUNK = 128
SGU_FFN = 4 * D_MODEL
SGU_HALF = SGU_FFN // 2
SGU_GROUPS = 8
SGU_GROUP_DIM = SGU_HALF // SGU_GROUPS

FFN_DIM = 2816
FFN_CONV_K = 3

NORM_EPS = 1e-6
NEG_INF = -1e30

kernel_name = "hybrid_swa_conformer_sgu_trunk"


def rmsnorm(x, g):
    xf = x.astype(jnp.float32)
    y = xf * lax.rsqrt(jnp.mean(xf * xf, axis=-1, keepdims=True) + NORM_EPS)
    return (y * g.astype(jnp.float32)).astype(x.dtype)


def layernorm(x, g, b):
    xf = x.astype(jnp.float32)
    mu = jnp.mean(xf, axis=-1, keepdims=True)
    xc = xf - mu
    var = jnp.mean(xc * xc, axis=-1, keepdims=True)
    y = xc * lax.rsqrt(var + NORM_EPS)
    return (y * g.astype(jnp.float32) + b.astype(jnp.float32)).astype(x.dtype)


def causal_dwconv(x, w, b):
    K, C = w.shape
    y = lax.conv_general_dilated(
        x, w[:, None, :].astype(x.dtype), window_strides=(1,), padding=[(K - 1, 0)],
        dimension_numbers=("NWC", "WIO", "NWC"), feature_group_count=C)
    return y + b


def alibi_slopes(n):
    return 2.0 ** (-8.0 * jnp.arange(1, n + 1, dtype=jnp.float32) / n)


def sliding_window_attention(h, wqkv, bqkv, sinks, wo, bo):
    B, T, _ = h.shape
    nb = T // BLOCK
    qkv = h @ wqkv + bqkv
    q, k, v = jnp.split(qkv, [N_HEADS * HEAD_DIM, (N_HEADS + N_KV) * HEAD_DIM], axis=-1)
    q = q.reshape(B, nb, BLOCK, N_KV, GROUP, HEAD_DIM)
    k = k.reshape(B, nb, BLOCK, N_KV, HEAD_DIM)
    v = v.reshape(B, nb, BLOCK, N_KV, HEAD_DIM)

    def with_prev(z):
        prev = jnp.pad(z, ((0, 0), (1, 0), (0, 0), (0, 0), (0, 0)))[:, :-1]
        return jnp.concatenate([prev, z], axis=2)

    k2, v2 = with_prev(k), with_prev(v)
    s = jnp.einsum("bnqkgd,bnskd->bnkgqs", q, k2).astype(jnp.float32) * (HEAD_DIM ** -0.5)

    qpos = jnp.arange(BLOCK) + BLOCK
    kpos = jnp.arange(2 * BLOCK)
    dist = qpos[:, None] - kpos[None, :]
    band = (dist >= 0) & (dist < WINDOW)
    blk = jnp.arange(nb)
    valid = band[None] & ~((blk[:, None, None] == 0) & (kpos[None, None, :] < BLOCK))

    slopes = alibi_slopes(N_HEADS).reshape(N_KV, GROUP)
    s = s - slopes[:, :, None, None] * dist.astype(jnp.float32)[None, None]
    s = jnp.where(valid[None, :, None, None], s, NEG_INF)

    sink = jnp.broadcast_to(sinks.astype(jnp.float32).reshape(N_KV, GROUP)[None, None, :, :, None, None],
                            s.shape[:-1] + (1,))
    p = jax.nn.softmax(jnp.concatenate([s, sink], axis=-1), axis=-1)[..., :-1]
    o = jnp.einsum("bnkgqs,bnskd->bnqkgd", p.astype(v2.dtype), v2).reshape(B, T, N_HEADS * HEAD_DIM)
    return o @ wo + bo


def conformer_conv(h, w_in, b_in, dw, dw_b, ln_g, ln_b, w_out, b_out):
    a, g = jnp.split(h @ w_in + b_in, 2, axis=-1)
    z = a * jax.nn.sigmoid(g)
    z = causal_dwconv(z, dw, dw_b)
    z = jax.nn.silu(layernorm(z, ln_g, ln_b))
    return z @ w_out + b_out


def chunked_sgu(h, w_in, b_in, ln_g, ln_b, ws, bs, w_out, b_out):
    B, T, _ = h.shape
    z = jax.nn.gelu(h @ w_in + b_in, approximate=False)
    u, v = jnp.split(z, 2, axis=-1)
    v = layernorm(v, ln_g, ln_b)
    v = v.reshape(B, T // CHUNK, CHUNK, SGU_GROUPS, SGU_GROUP_DIM)
    mask = jnp.tril(jnp.ones((CHUNK, CHUNK), dtype=bool))
    wm = jnp.where(mask[None], ws, jnp.zeros_like(ws))
    v = jnp.einsum("gts,bcsgd->bctgd", wm.astype(v.dtype), v) + bs.T[None, None, :, :, None]
    v = v.reshape(B, T, SGU_HALF)
    return (u * v) @ w_out + b_out


def conv_glu_ffn(h, w_in, dw, dw_b, w_out):
    z = causal_dwconv(h @ w_in, dw, dw_b)
    g, u = jnp.split(z, 2, axis=-1)
    return (jax.nn.silu(g) * u) @ w_out


def setup_inputs(seed: int = 0) -> dict:
    key = jax.random.key(seed)
    ks = jax.random.split(key, 40)
    D = D_MODEL

    def nrm(k, shape, scale):
        return jax.random.normal(k, shape, jnp.float32) * scale

    return {
        "x": nrm(ks[0], (BATCH, SEQ, D), 1.0),
        "c": nrm(ks[1], (BATCH, D), 1.0),
        "norm1_g": 1.0 + nrm(ks[2], (DEPTH, D), 0.05),
        "norm2_g": 1.0 + nrm(ks[3], (DEPTH, D), 0.05),
        "ada_w": nrm(ks[4], (DEPTH, D, 6 * D), 0.5 * D ** -0.5),
        "ada_b": nrm(ks[5], (DEPTH, 6 * D), 0.01),
        "attn_wqkv": nrm(ks[6], (N_A, D, QKV_DIM), D ** -0.5),
        "attn_bqkv": nrm(ks[7], (N_A, QKV_DIM), 0.01),
        "attn_sinks": nrm(ks[8], (N_A, N_HEADS), 0.5),
        "attn_wo": nrm(ks[9], (N_A, N_HEADS * HEAD_DIM, D), (N_HEADS * HEAD_DIM) ** -0.5),
        "attn_bo": nrm(ks[10], (N_A, D), 0.01),
        "conv_w_in": nrm(ks[11], (N_B, D, 2 * D), D ** -0.5),
        "conv_b_in": nrm(ks[12], (N_B, 2 * D), 0.01),
        "conv_dw": nrm(ks[13], (N_B, CONV_K, D), CONV_K ** -0.5),
        "conv_dw_b": nrm(ks[14], (N_B, D), 0.01),
        "conv_ln_g": 1.0 + nrm(ks[15], (N_B, D), 0.05),
        "conv_ln_b": nrm(ks[16], (N_B, D), 0.01),
        "conv_w_out": nrm(ks[17], (N_B, D, D), D ** -0.5),
        "conv_b_out": nrm(ks[18], (N_B, D), 0.01),
        "sgu_w_in": nrm(ks[19], (N_C, D, SGU_FFN), D ** -0.5),
        "sgu_b_in": nrm(ks[20], (N_C, SGU_FFN), 0.01),
        "sgu_ln_g": 1.0 + nrm(ks[21], (N_C, SGU_HALF), 0.05),
        "sgu_ln_b": nrm(ks[22], (N_C, SGU_HALF), 0.01),
        "sgu_ws": nrm(ks[23], (N_C, SGU_GROUPS, CHUNK, CHUNK), CHUNK ** -0.5),
        "sgu_bs": 1.0 + nrm(ks[24], (N_C, SGU_GROUPS, CHUNK), 0.01),
        "sgu_w_out": nrm(ks[25], (N_C, SGU_HALF, D), SGU_HALF ** -0.5),
        "sgu_b_out": nrm(ks[26], (N_C, D), 0.01),
        "ffn_w_in": nrm(ks[27], (DEPTH, D, 2 * FFN_DIM), D ** -0.5),
        "ffn_dw": nrm(ks[28], (DEPTH, FFN_CONV_K, 2 * FFN_DIM), FFN_CONV_K ** -0.5),
        "ffn_dw_b": nrm(ks[29], (DEPTH, 2 * FFN_DIM), 0.01),
        "ffn_w_out": nrm(ks[30], (DEPTH, FFN_DIM, D), FFN_DIM ** -0.5),
        "final_g": 1.0 + nrm(ks[31], (D,), 0.05),
    }


def reference(x, c, norm1_g, norm2_g, ada_w, ada_b,
              attn_wqkv, attn_bqkv, attn_sinks, attn_wo, attn_bo,
              conv_w_in, conv_b_in, conv_dw, conv_dw_b, conv_ln_g, conv_ln_b, conv_w_out, conv_b_out,
              sgu_w_in, sgu_b_in, sgu_ln_g, sgu_ln_b, sgu_ws, sgu_bs, sgu_w_out, sgu_b_out,
              ffn_w_in, ffn_dw, ffn_dw_b, ffn_w_out, final_g):
    c_act = jax.nn.silu(c)
    for i in range(DEPTH):
        mod = (c_act @ ada_w[i] + ada_b[i])[:, None, :]
        sh1, sc1, g1, sh2, sc2, g2 = jnp.split(mod, 6, axis=-1)

        h = rmsnorm(x, norm1_g[i]) * (1.0 + sc1) + sh1
        kind, j = i % N_MIXERS, i // N_MIXERS
        if kind == 0:
            y = sliding_window_attention(h, attn_wqkv[j], attn_bqkv[j], attn_sinks[j], attn_wo[j], attn_bo[j])
        elif kind == 1:
            y = conformer_conv(h, conv_w_in[j], conv_b_in[j], conv_dw[j], conv_dw_b[j],
                               conv_ln_g[j], conv_ln_b[j], conv_w_out[j], conv_b_out[j])
        else:
            y = chunked_sgu(h, sgu_w_in[j], sgu_b_in[j], sgu_ln_g[j], sgu_ln_b[j],
                            sgu_ws[j], sgu_bs[j], sgu_w_out[j], sgu_b_out[j])
        x = x + g1 * y

        h = rmsnorm(x, norm2_g[i]) * (1.0 + sc2) + sh2
        x = x + g2 * conv_glu_ffn(h, ffn_w_in[i], ffn_dw[i], ffn_dw_b[i], ffn_w_out[i])
    return rmsnorm(x, final_g)
```

```python
import contextlib
import numpy as np
import concourse.bass as bass
import concourse.mybir as mybir
from concourse.bass_utils import run_bass_kernel_spmd

F32 = mybir.dt.float32
BF16 = mybir.dt.bfloat16
AF = mybir.ActivationFunctionType
ALU = mybir.AluOpType

D = 1024
KC = 8
NL = 4
FFN = 2816
FJ = 22
NBLK_FULL = 34
SEQ = 8192
EPS = 1e-6
COMPUTE = ("pe", "act", "dve", "pool")


class Res:
    __slots__ = ("name", "last_write", "reads", "uid", "excl")
    _n = [0]

    def __init__(self, name=""):
        self.name = name
        self.excl = False
        self.last_write = None
        self.reads = []
        Res._n[0] += 1
        self.uid = Res._n[0]


class Op:
    __slots__ = ("eng", "fn", "deps", "signal", "sem", "sigval", "stream")

    def __init__(self, eng, fn, stream=None):
        self.eng = eng
        self.fn = fn
        self.deps = []
        self.signal = False
        self.sem = None
        self.sigval = None
        self.stream = stream


class Prog:
    def __init__(self, nc):
        self.nc = nc
        self.ops = {e: [] for e in COMPUTE + ("sp",)}
        self.phase = 0
        self.pending = {e: [] for e in COMPUTE + ("sp",)}
        self.last_stream = {}
        self.dma_slots = {}
        self.all_ops = []

    def barrier(self, new_phase=True):
        lasts = []
        for e, lst in self.ops.items():
            for o in reversed(lst):
                if o.stream is None:
                    lasts.append(o)
                    break
        lasts += list(self.last_stream.values())
        for e in self.pending:
            self.pending[e] = list(lasts)
        if new_phase:
            self.phase += 1
            self.dma_slots = {}

    def op(self, eng, fn, reads=(), writes=(), stream=None):
        if stream is not None:
            stream = self.dma_slots.setdefault(stream.uid, len(self.dma_slots))
        o = Op(eng, fn, stream)
        o.sem = ("dma", stream) if stream is not None else (eng, self.phase)
        if stream is not None:
            o.signal = True
            self.last_stream[stream] = o
        deps = []
        for r in reads:
            if r.last_write is not None:
                deps.append((r.last_write, "raw"))
            if r.excl:
                for rd in r.reads:
                    if rd.eng != eng:
                        deps.append((rd, "rar"))
        for w in writes:
            if w.last_write is not None:
                deps.append((w.last_write, "waw"))
            for rd in w.reads:
                deps.append((rd, "war"))
        for d in self.pending[eng]:
            deps.append((d, "bar"))
        self.pending[eng] = []
        seen = set()
        for d, kind in deps:
            if d is o or id(d) in seen:
                continue
            same = (d.eng == eng) and d.stream is None and stream is None
            if same:
                if eng == "pe":
                    continue
                if kind not in ("raw",):
                    continue
            seen.add(id(d))
            d.signal = True
            o.deps.append(d)
        for r in reads:
            r.reads.append(o)
        for w in writes:
            w.last_write = o
            w.reads = []
        self.ops[eng].append(o)
        self.all_ops.append(o)
        return o

    def emit(self):
        nc = self.nc
        counts = {}
        for o in self.all_ops:
            if o.signal:
                inc = 16 if o.stream is not None else 1
                counts[o.sem] = counts.get(o.sem, 0) + inc
                o.sigval = counts[o.sem]
        sem_keys = sorted(counts.keys(), key=str)
        with contextlib.ExitStack() as es:
            sems = {}
            for k in sem_keys:
                sems[k] = es.enter_context(nc.semaphore("s_%s_%s" % (k[0], k[1])))
            block = es.enter_context(nc.Block())

            def run(ename):
                def _f(eng):
                    waited = {}
                    for o in self.ops[ename]:
                        need = {}
                        for d in o.deps:
                            if need.get(d.sem, 0) < d.sigval:
                                need[d.sem] = d.sigval
                        for sk_, v_ in need.items():
                            if waited.get(sk_, 0) < v_:
                                eng.wait_ge(sems[sk_], v_)
                                waited[sk_] = v_
                        ins = o.fn(eng)
                        if o.signal:
                            ins.then_inc(sems[o.sem], 16 if o.stream is not None else 1)
                    if ename == "sp":
                        for k in sem_keys:
                            if k[0] == "dma":
                                eng.wait_ge(sems[k], counts[k])
                return _f

            block.sync(run("sp"))
            block.tensor(run("pe"))
            block.scalar(run("act"))
            block.vector(run("dve"))
            block.gpsimd(run("pool"))


class Buf:
    __slots__ = ("ap", "r")

    def __init__(self, ap, name=""):
        self.ap = ap
        self.r = Res(name)


class Arena:
    def __init__(self, base_ap, words):
        self.base = base_ap
        self.words = words
        self.off = 0

    def reset(self):
        self.off = 0

    def alloc(self, name, shape, dt):
        assert shape[0] == 128
        n = 1
        for s in shape[1:]:
            n *= s
        nbytes = n * (2 if dt == BF16 else 4)
        words = (nbytes + 3) // 4
        words = (words + 7) // 8 * 8
        assert self.off + words <= self.words, "arena overflow %s: %d + %d > %d" % (name, self.off, words, self.words)
        v = self.base[:, self.off:self.off + words]
        self.off += words
        if dt == BF16:
            v = v.bitcast(BF16)
        v = v[:, 0:n]
        if len(shape) == 3:
            v = v.rearrange("p (a b) -> p a b", a=shape[1])
        elif len(shape) == 4:
            v = v.rearrange("p (a b c) -> p a b c", a=shape[1], b=shape[2])
        return Buf(v, name)


def tiles_of(nblk, nb):
    t = []
    b = 0
    while b < nblk:
        n = min(nb, nblk - b)
        t.append((b, n))
        b += n
    return t


def build_program(NBLK=NBLK_FULL, passes=None, debug_out=False):
    if passes is None:
        passes = ["a0", "f0", "c1", "f1", "s2", "f2", "a3", "f3"]
    nc = bass.Bass("TRN2", target_bir_lowering=False)
    T = NBLK * 128

    def din(name, shape):
        return nc.dram_tensor(name, list(shape), F32, kind="ExternalInput").ap()

    x_in = din("x", [T, D])
    ccol = din("ccol", [128, KC])
    adaw = din("adaw", [NL, 6, 128, KC, D])
    adab_col = din("adab_col", [128, NL, 6, KC])
    adab_row = din("adab_row", [NL, 6, D])
    ng_col = din("ng_col", [128, NL, 2, KC])
    final_g = din("final_g", [1, D])
    ident_d = din("ident", [128, 128])
    tri_d = din("tri", [128, 128])
    E_cur_d = din("E_cur", [128, 2048])
    E_prev_d = din("E_prev", [128, 2048])
    wqk_d = din("wqk", [2, 128, KC, 1280])
    wv_d = din("wv", [2, 128, KC, 256])
    bqk_col_d = din("bqk_col", [128, 2, 10])
    bv_d = din("bv", [2, 1, 256])
    wo_d = din("wo", [2, 128, KC, D])
    bo_d = din("bo", [2, 1, D])
    sinks_d = din("sinks", [2, 1, 16])
    cw_in_d = din("cw_in", [128, KC, 2048])
    cb_in_col_d = din("cb_in_col", [128, 16])
    cdw_col_d = din("cdw_col", [128, 31, KC])
    cdwb_col_d = din("cdwb_col", [128, KC])
    clng_col_d = din("clng_col", [128, KC])
    clnb_col_d = din("clnb_col", [128, KC])
    cw_out_d = din("cw_out", [128, KC, D])
    cb_out_d = din("cb_out", [1, D])
    sw_in_d = din("sw_in", [128, KC, 4096])
    sb_in_ucol_d = din("sb_in_ucol", [128, 16])
    sb_in_v_d = din("sb_in_v", [1, 2048])
    slng_d = din("slng", [1, 2048])
    slnb_d = din("slnb", [1, 2048])
    swsT_d = din("swsT", [128, 8, 128])
    sbs_d = din("sbs", [1, 8 * 128])
    sw_out_d = din("sw_out", [128, 16, D])
    sb_out_d = din("sb_out", [1, D])
    fw_in_d = din("fw_in", [NL, 128, KC, 2 * FFN])
    fdw_col_d = din("fdw_col", [128, NL, 3, 44])
    fdwb_col_d = din("fdwb_col", [128, NL, 44])
    fw_out_d = din("fw_out", [NL, 128, FJ, D])

    out_d = nc.dram_tensor("out", [T, D], F32, kind="ExternalOutput").ap()
    xs_d = nc.dram_tensor("xs", [T, D], F32).ap()
    grow_d = nc.dram_tensor("grow", [NL * 2, D], F32).ap()

    P = Prog(nc)
    es = contextlib.ExitStack()

    def sb(name, shape, dt):
        return Buf(es.enter_context(nc.sbuf_tensor(name, list(shape), dt)), name)

    AW = 50500
    arena_t = es.enter_context(nc.sbuf_tensor("arena", [128, AW], F32))
    A = Arena(arena_t[:, :], AW)

    ident_f = sb("ident_f", [128, 128], F32)
    ident_b = sb("ident_b", [128, 128], BF16)
    ones_b = sb("ones_b", [128, 128], BF16)
    onesN_b = sb("onesN_b", [128, 128], BF16)
    cact = sb("cact", [128, KC], F32)
    modc = sb("modc", [128, NL, 6, KC], F32)
    gsc = sb("gsc", [128, NL, 2, KC], F32)
    ngc = sb("ngc", [128, NL, 2, KC], F32)
    adabc = sb("adabc", [128, NL, 6, KC], F32)
    ssq = sb("ssq", [128, NBLK], F32)
    rstd = sb("rstd", [128, NBLK], F32)
    junk = sb("junk", [128, D], F32)

    banksT = [Buf(es.enter_context(nc.psum_tensor("pT%d" % i, [128, 1024], BF16)), "pT%d" % i) for i in range(2)]
    banks = [Buf(es.enter_context(nc.psum_tensor("pb%d" % i, [128, 512], F32)), "pb%d" % i) for i in range(6)]
    for b_ in banksT + banks:
        b_.r.excl = True
    bank_ctr = [0]
    bankT_ctr = [0]

    bank_excl = []

    def bank():
        while True:
            b = banks[bank_ctr[0] % 6]
            bank_ctr[0] += 1
            if not any(b is x for x in bank_excl):
                return b

    def end_setup(mark):
        P.barrier(new_phase=False)
        A.off = mark

    def bankT():
        b = banksT[bankT_ctr[0] % 2]
        bankT_ctr[0] += 1
        return b

    def R(bufs):
        return [b.r if isinstance(b, Buf) else b for b in bufs]

    def dma(eng, out, in_, reads, writes, stream=None):
        key = R(writes)[0] if writes else R(reads)[0]
        P.op(eng, lambda e: e.dma_start(out=out, in_=in_), R(reads), R(writes), stream=key)

    def mm(out, lhsT, rhs, start, stop, reads, writes):
        P.op("pe", lambda e: e.matmul(out, lhsT=lhsT, rhs=rhs, start=start, stop=stop), R(reads), R(writes))

    def tr(out, in_, ident, reads, writes):
        P.op("pe", lambda e: e.transpose(out=out, in_=in_, identity=ident), R(reads), R(writes))

    def act(out, in_, func, reads, writes, **kw):
        P.op("act", lambda e: e.activation(out=out, in_=in_, func=func, **kw), R(reads), R(writes))

    def tt(eng, out, in0, in1, op, reads, writes):
        P.op(eng, lambda e: e.tensor_tensor(out=out, in0=in0, in1=in1, op=op), R(reads), R(writes))

    def ts(eng, out, in0, s1, s2, op0, op1, reads, writes):
        if s2 is None:
            P.op(eng, lambda e: e.tensor_scalar(out=out, in0=in0, scalar1=s1, scalar2=None, op0=op0), R(reads), R(writes))
        else:
            P.op(eng, lambda e: e.tensor_scalar(out=out, in0=in0, scalar1=s1, scalar2=s2, op0=op0, op1=op1),
                 R(reads), R(writes))

    def stt(eng, out, in0, scalar, in1, op0, op1, reads, writes):
        P.op(eng, lambda e: e.scalar_tensor_tensor(out=out, in0=in0, scalar=scalar, in1=in1, op0=op0, op1=op1),
             R(reads), R(writes))

    def cp(eng, out, in_, reads, writes):
        P.op(eng, lambda e: e.tensor_copy(out=out, in_=in_), R(reads), R(writes))

    def mset(eng, ap, val, writes):
        P.op(eng, lambda e: e.memset(ap, val), [], R(writes))

    dma("sp", ident_f.ap[:], ident_d, [], [ident_f], "ld")
    dma("sp", cact.ap[:], ccol, [], [cact], "ld")
    dma("sp", adabc.ap[:], adab_col, [], [adabc], "ld")
    dma("sp", ngc.ap[:], ng_col, [], [ngc], "ld")
    cp("dve", ident_b.ap[:], ident_f.ap[:], [ident_f], [ident_b])
    mset("pool", ones_b.ap[:], 1.0, [ones_b])
    mset("pool", onesN_b.ap[:], 1.0 / 1024.0, [onesN_b])
    mset("pool", ssq.ap[:], 0.0, [ssq])
    act(cact.ap[:], cact.ap[:], AF.Silu, [cact], [cact])

    slabs = [A.alloc("slab%d" % i, [128, KC, D], F32) for i in range(2)]
    arow = A.alloc("arow", [128, D], F32)
    grow_sb = A.alloc("grow_sb", [128, D], F32)
    mrow_d = nc.dram_tensor("mrow", [16, D], F32).ap()
    mrow_res = Res("mrow_d")
    si = 0
    for l in range(NL):
        for j in range(6):
            slab = slabs[si % 2]
            si += 1
            dma("sp", slab.ap[:], adaw[l, j], [], [slab], "ld")
            dma("sp", arow.ap[0:1, :], adab_row[l, j:j + 1, :], [], [arow], "ld")
            pbs = [bank(), bank()]
            for h in range(2):
                for k in range(KC):
                    mm(pbs[h].ap[0:1, :], cact.ap[:, k:k + 1], slab.ap[:, k, h * 512:(h + 1) * 512],
                       k == 0, k == KC - 1, [slab, cact], [pbs[h]])
            for h in range(2):
                tt("dve", grow_sb.ap[0:1, h * 512:(h + 1) * 512], pbs[h].ap[0:1, :], arow.ap[0:1, h * 512:(h + 1) * 512],
                   ALU.add, [pbs[h], arow], [grow_sb])
            if j in (2, 5):
                gi = l * 2 + (0 if j == 2 else 1)
                dma("sp", grow_d[gi:gi + 1, :], grow_sb.ap[0:1, :], [grow_sb], [], "st")
            else:
                r_ = l * 4 + {0: 0, 1: 1, 3: 2, 4: 3}[j]
                P.op("sp", lambda e, r_=r_: e.dma_start(out=mrow_d[r_:r_ + 1, :], in_=grow_sb.ap[0:1, :]),
                     [grow_sb.r], [mrow_res], stream=grow_sb.r)
    mst = A.alloc("mst", [128, 128], F32)
    P.op("sp", lambda e: e.dma_start(out=mst.ap[:], in_=mrow_d.rearrange("r (c p) -> (r c) p", p=128)),
         [mrow_res], [mst.r], stream=mst.r)
    pbm = bank()
    tr(pbm.ap[:, 0:128], mst.ap[:], ident_f.ap[:], [mst, ident_f], [pbm])
    for l in range(NL):
        for jj, j in enumerate((0, 1, 3, 4)):
            cp("dve", modc.ap[:, l, j, :], pbm.ap[:, (l * 4 + jj) * 8:(l * 4 + jj + 1) * 8], [pbm], [modc])
    for l in range(NL):
        for s in range(2):
            stt("dve", gsc.ap[:, l, s, :], modc.ap[:, l, 1 + 3 * s, :], 1.0, ngc.ap[:, l, s, :], ALU.add, ALU.mult,
                [modc, ngc], [gsc])

    P.barrier()
    A.reset()
    NB = 2
    tiles = tiles_of(NBLK, NB)
    xb0 = [A.alloc("xb%d" % i, [128, NB, D], F32) for i in range(2)]
    for ti, (b0, nb) in enumerate(tiles):
        xt = xb0[ti % 2]
        dma("sp", xt.ap[:, 0:nb, :], x_in[b0 * 128:(b0 + nb) * 128, :].rearrange("(b p) d -> p b d", p=128), [], [xt], "ld")
        for b in range(nb):
            act(junk.ap[:], xt.ap[:, b, :], AF.Square, [xt], [junk, ssq], accum_out=ssq.ap[:, b0 + b:b0 + b + 1])

    state = {"src": x_in}

    def pass_begin():
        P.barrier()
        A.reset()
        ts("dve", rstd.ap[:], ssq.ap[:], 1.0 / D, EPS, ALU.mult, ALU.add, [ssq], [rstd])
        act(rstd.ap[:], rstd.ap[:], AF.Sqrt, [rstd], [rstd])
        P.op("dve", lambda e: e.reciprocal(out=rstd.ap[:], in_=rstd.ap[:]), [rstd.r], [rstd.r])
        mset("pool", ssq.ap[:], 0.0, [ssq])

    def load_x(xt, b0, nb):
        src = state["src"]
        dma("sp", xt.ap[:, 0:nb, :], src[b0 * 128:(b0 + nb) * 128, :].rearrange("(b p) d -> p b d", p=128), [], [xt], "ld")

    def store_x(xt, b0, nb, dst):
        dma("sp", dst[b0 * 128:(b0 + nb) * 128, :].rearrange("(b p) d -> p b d", p=128), xt.ap[:, 0:nb, :], [xt], [], "st")

    def prologue_a(xt, xn, b0, nb):
        for b in range(nb):
            act(xn.ap[:, b, :], xt.ap[:, b, :], AF.Copy, [xt, rstd], [xn], scale=rstd.ap[:, b0 + b:b0 + b + 1])

    def prologue_b(xn, hT, nb, l, s):
        W = nb * 128
        cpb = 1024 // W
        for k0 in range(0, KC, cpb):
            pt = bankT()
            for kk in range(cpb):
                k = k0 + kk
                for b in range(nb):
                    tr(pt.ap[:, kk * W + b * 128:kk * W + (b + 1) * 128], xn.ap[:, b, k * 128:(k + 1) * 128], ident_b.ap[:],
                       [xn, ident_b], [pt])
            for kk in range(cpb):
                k = k0 + kk
                ts("dve", hT.ap[:, k, 0:W], pt.ap[:, kk * W:(kk + 1) * W], gsc.ap[:, l, s, k:k + 1],
                   modc.ap[:, l, 3 * s, k:k + 1], ALU.mult, ALU.add, [pt, gsc, modc], [hT])

    def prologue(xt, xn, hT, b0, nb, l, s):
        prologue_a(xt, xn, b0, nb)
        prologue_b(xn, hT, nb, l, s)

    def out_proj_pipelined(ti, tiles, xbs, xn, hTs, l, s, do_block):
        b0, nb = tiles[ti]
        nxt = tiles[ti + 1] if ti + 1 < len(tiles) else None
        for b in range(nb):
            if nxt is not None:
                if b == 0:
                    prologue_a(xbs[(ti + 1) % 2], xn, nxt[0], nxt[1])
                if b == nb - 1:
                    prologue_b(xn, hTs[(ti + 1) % 2], nxt[1], l, s)
            do_block(b)

    def epilogue_block(xt, b, blk, pbs):
        for h in range(2):
            tt("dve", xt.ap[:, b, h * 512:(h + 1) * 512], pbs[h].ap[:, :], xt.ap[:, b, h * 512:(h + 1) * 512], ALU.add,
               [pbs[h], xt], [xt])
        act(junk.ap[:], xt.ap[:, b, :], AF.Square, [xt], [junk, ssq], accum_out=ssq.ap[:, blk:blk + 1])

    def load_gbc(gi):
        g = A.alloc("gbc", [128, D], F32)
        dma("sp", g.ap[:], grow_d[gi:gi + 1, :].partition_broadcast(128), [], [g], "ld")
        return g

    def load_w_scaled(dst, src_d, nchunks, gbc, name):
        stg = [A.alloc("%s_stg%d" % (name, i), [128, D], F32) for i in range(2)]
        for j in range(nchunks):
            s_ = stg[j % 2]
            dma("sp", s_.ap[:], src_d[:, j, :], [], [s_], "ld")
            tt("pool", dst.ap[:, j, :], s_.ap[:], gbc.ap[:], ALU.mult, [s_, gbc], [dst])

    def bias_row_scaled(b_d, gbc, name, rb):
        r32 = A.alloc(name + "_r32", [128, D], F32)
        dma("sp", r32.ap[0:1, :], b_d, [], [r32], "ld")
        tt("dve", rb.ap[0:1, :], r32.ap[0:1, :], gbc.ap[0:1, :], ALU.mult, [r32, gbc], [rb])
        return rb

    def out_proj_block(xt, b, blk, lhs_buf, nk, w_buf, brow, lhs_res=None):
        pbs = [bank(), bank()]
        for h in range(2):
            first = True
            if brow is not None:
                mm(pbs[h].ap[:, :], ones_b.ap[0:1, :], brow.ap[0:1, h * 512:(h + 1) * 512], True, False,
                   [ones_b, brow], [pbs[h]])
                first = False
            for k in range(nk):
                mm(pbs[h].ap[:, :], lhs_buf.ap[:, k, b * 128:(b + 1) * 128], w_buf.ap[:, k, h * 512:(h + 1) * 512],
                   first and k == 0, k == nk - 1, [lhs_res[k] if lhs_res is not None else lhs_buf, w_buf], [pbs[h]])
        epilogue_block(xt, b, blk, pbs)

    def ffn_pass(l, dst):
        pass_begin()
        NB = 2
        WMAX = NB * 128
        w_in = A.alloc("fw_in", [128, KC, 2 * FFN], BF16)
        w_out = A.alloc("fw_out", [128, FJ, D], BF16)
        dwc = A.alloc("fdwc", [128, 3, 44], F32)
        dwb = A.alloc("fdwb", [128, 44], F32)
        carry = A.alloc("fcarry", [128, FJ, 2, 2], F32)
        cc = A.alloc("fcc", [128, FJ, 2, 2], F32)
        ct1 = A.alloc("fct1", [128, FJ, 2], F32)
        ct2 = A.alloc("fct2", [128, FJ, 2], F32)
        mark = A.off
        gbc = load_gbc(l * 2 + 1)
        wgr = [Res("fw_in_g%d" % i) for i in range(4)]
        wbounds = [0, 6, 12, 17, FJ]
        wgrp = {}
        for gi_ in range(4):
            j0_, j1_ = wbounds[gi_], wbounds[gi_ + 1]
            for j_ in range(j0_, j1_):
                wgrp[j_] = wgr[gi_]
            for part in range(2):
                c0_, c1_ = part * FFN + j0_ * 128, part * FFN + j1_ * 128
                dma("pool", w_in.ap[:, :, c0_:c1_], fw_in_d[l, :, :, c0_:c1_], [], [wgr[gi_]], "wld")
        dma("sp", dwc.ap[:], fdw_col_d[:, l], [], [dwc], "ld")
        dma("sp", dwb.ap[:], fdwb_col_d[:, l], [], [dwb], "ld")
        mset("pool", carry.ap[:], 0.0, [carry])
        load_w_scaled(w_out, fw_out_d[l], FJ, gbc, "fwo")
        end_setup(mark)
        xbs = [A.alloc("xb%d" % i, [128, NB, D], F32) for i in range(2)]
        xn = A.alloc("xn", [128, NB, D], BF16)
        hTs = [A.alloc("hT%d" % i, [128, KC, WMAX], BF16) for i in range(2)]
        aT = A.alloc("aT", [128, FJ, WMAX], BF16)
        aTr = [Res("aT%d" % i) for i in range(FJ)]
        NY = 6
        ybufs = [A.alloc("y%d" % i, [128, 2, WMAX], F32) for i in range(NY)]
        yres = [(Res("yg%d" % i), Res("yu%d" % i)) for i in range(NY)]
        sbufs = [A.alloc("s%d" % i, [128, WMAX], F32) for i in range(3)]
        tubufs = [A.alloc("tu%d" % i, [128, WMAX], F32) for i in range(3)]
        dw0v = dwc.ap[:, 0, :].rearrange("p (a j) -> p j a", a=2)
        dw1v = dwc.ap[:, 1, :].rearrange("p (a j) -> p j a", a=2)
        tiles = tiles_of(NBLK, NB)
        load_x(xbs[0], *tiles[0])
        yc = 0
        for ti, (b0, nb) in enumerate(tiles):
            W = nb * 128
            assert W == 256
            xt = xbs[ti % 2]
            hT = hTs[ti % 2]
            if ti + 1 < len(tiles):
                load_x(xbs[(ti + 1) % 2], *tiles[ti + 1])
            if ti == 0:
                prologue(xt, xn, hT, b0, nb, l, 1)
            tt("pool", ct1.ap[:], carry.ap[:, :, :, 0], dw0v, ALU.mult, [carry, dwc], [ct1])
            tt("pool", ct2.ap[:], carry.ap[:, :, :, 1], dw1v, ALU.mult, [carry, dwc], [ct2])
            tt("pool", cc.ap[:, :, :, 0], ct1.ap[:], ct2.ap[:], ALU.add, [ct1, ct2], [cc])
            tt("pool", cc.ap[:, :, :, 1], carry.ap[:, :, :, 1], dw0v, ALU.mult, [carry, dwc], [cc])
            tails = []

            def ffn_tail(j_, y_, s__, yr_, W=W):
                act(s__.ap[:, 0:W], y_.ap[:, 0, 0:W], AF.Silu, [yr_[0]], [s__])
                tt("pool", aT.ap[:, j_, 0:W], s__.ap[:, 0:W], y_.ap[:, 1, 0:W], ALU.mult, [s__, yr_[1]], [aTr[j_]])

            for j in range(FJ):
                pb = bank()
                for part in range(2):
                    off = part * FFN + j * 128
                    for k in range(KC):
                        mm(pb.ap[:, part * W:(part + 1) * W], w_in.ap[:, k, off:off + 128], hT.ap[:, k, 0:W], k == 0, k == KC - 1,
                           [wgrp[j], hT], [pb])
                y = ybufs[yc % NY]
                yr = yres[yc % NY]
                s_ = sbufs[yc % 3]
                yc += 1
                for part in range(2):
                    c = part * FJ + j
                    o_ = part * W
                    act(y.ap[:, part, 0:W], pb.ap[:, o_:o_ + W], AF.Identity, [pb, dwc, dwb], [yr[part]],
                        scale=dwc.ap[:, 2, c:c + 1], bias=dwb.ap[:, c:c + 1])
                act(carry.ap[:, j, :, :], pb.ap[:, :].rearrange("p (a w) -> p a w", a=2)[:, :, W - 2:W], AF.Copy, [pb], [carry])
                for tap, sh_ in ((1, 1), (0, 2)):
                    for part in range(2):
                        c = part * FJ + j
                        o_ = part * W
                        stt("dve", y.ap[:, part, sh_:W], pb.ap[:, o_:o_ + W - sh_], dwc.ap[:, tap, c:c + 1], y.ap[:, part, sh_:W],
                            ALU.mult, ALU.add, [pb, dwc, yr[part]], [yr[part]])
                tt("pool", y.ap[:, :, 0:2], y.ap[:, :, 0:2], cc.ap[:, j, :, :], ALU.add, [yr[0], yr[1], cc], [yr[0], yr[1]])
                tails.append((j, y, s_, yr))
                if len(tails) > 2:
                    ffn_tail(*tails.pop(0))
            while tails:
                ffn_tail(*tails.pop(0))
            out_proj_pipelined(ti, tiles, xbs, xn, hTs, l, 1,
                               lambda b, xt=xt, b0=b0: out_proj_block(xt, b, b0 + b, aT, FJ, w_out, None, aTr))
            store_x(xt, b0, nb, dst)
        state["src"] = dst

    def attn_pass(l, ja, dst):
        pass_begin()
        NB = 2
        WMAX = NB * 128
        wqk = A.alloc("wqk", [128, KC, 1280], BF16)
        wv = A.alloc("wv", [128, KC, 256], BF16)
        wo = A.alloc("wo", [128, KC, D], BF16)
        bqk = A.alloc("bqk", [128, 10], F32)
        bvb = A.alloc("bvb", [128, 256], F32)
        Ecur = A.alloc("Ecur", [128, 2048], F32)
        Eprev = A.alloc("Eprev", [128, 2048], F32)
        sk = A.alloc("sk", [128, 16], F32)
        sinkE = A.alloc("sinkE", [128, 16, 128], F32)
        brow = A.alloc("abo_rb", [128, D], BF16)
        mark = A.off
        gbc = load_gbc(l * 2 + 0)
        dma("pool", wqk.ap[:], wqk_d[ja], [], [wqk], "wld")
        dma("pool", wv.ap[:], wv_d[ja], [], [wv], "wld")
        dma("sp", bqk.ap[:], bqk_col_d[:, ja, :], [], [bqk], "ld")
        dma("sp", bvb.ap[:], bv_d[ja].partition_broadcast(128), [], [bvb], "ld")
        dma("sp", Ecur.ap[:], E_cur_d, [], [Ecur], "ld")
        dma("sp", Eprev.ap[:], E_prev_d, [], [Eprev], "ld")
        dma("sp", sk.ap[:], sinks_d[ja].partition_broadcast(128), [], [sk], "ld")
        act(sk.ap[:], sk.ap[:], AF.Exp, [sk], [sk])
        cp("dve", sinkE.ap[:], sk.ap[:, :].unsqueeze(2).to_broadcast([128, 16, 128]), [sk], [sinkE])
        load_w_scaled(wo, wo_d[ja], KC, gbc, "awo")
        bias_row_scaled(bo_d[ja], gbc, "abo", brow)
        end_setup(mark)
        xbs = [A.alloc("xb%d" % i, [128, NB, D], F32) for i in range(2)]
        xn = A.alloc("xn", [128, NB, D], BF16)
        hTs = [A.alloc("hT%d" % i, [128, KC, WMAX], BF16) for i in range(2)]
        qT = A.alloc("qT", [128, KC, WMAX], BF16)
        kT = A.alloc("kT", [128, 2, 128 + WMAX], BF16)
        V = A.alloc("V", [128, 1 + NB, 256], BF16)
        oT = A.alloc("oT", [128, KC, WMAX], BF16)
        ebufs = [A.alloc("e%d" % i, [128, 512], F32) for i in range(4)]
        pTs = [A.alloc("pT%d" % i, [128, 512], BF16) for i in range(6)]
        dens = [A.alloc("den%d" % i, [128, 512], F32) for i in range(2)]
        mset("pool", kT.ap[:], 0.0, [kT])
        mset("pool", V.ap[:], 0.0, [V])
        tiles = tiles_of(NBLK, NB)
        load_x(xbs[0], *tiles[0])
        ec = 0
        dcn = [0]
        for ti, (b0, nb) in enumerate(tiles):
            W = nb * 128
            xt = xbs[ti % 2]
            hT = hTs[ti % 2]
            if ti + 1 < len(tiles):
                load_x(xbs[(ti + 1) % 2], *tiles[ti + 1])
            if ti == 0:
                prologue(xt, xn, hT, b0, nb, l, 0)
            for c in range(10):
                pb = bank()
                for k in range(KC):
                    mm(pb.ap[:, 0:W], wqk.ap[:, k, c * 128:(c + 1) * 128], hT.ap[:, k, 0:W], k == 0, k == KC - 1,
                       [wqk, hT], [pb])
                if c < 8:
                    act(qT.ap[:, c, 0:W], pb.ap[:, 0:W], AF.Identity, [pb, bqk], [qT], bias=bqk.ap[:, c:c + 1])
                else:
                    act(kT.ap[:, c - 8, 128:128 + W], pb.ap[:, 0:W], AF.Identity, [pb, bqk], [kT], bias=bqk.ap[:, c:c + 1])
            for b in range(nb):
                pb = bank()
                for k in range(KC):
                    mm(pb.ap[:, 0:256], hT.ap[:, k, b * 128:(b + 1) * 128], wv.ap[:, k, :], k == 0, k == KC - 1,
                       [hT, wv], [pb])
                tt("dve", V.ap[:, 1 + b, :], pb.ap[:, 0:256], bvb.ap[:], ALU.add, [pb, bvb], [V])
            def attn_back(b, kv, pts):
                p_, e_ = kv // 2, kv % 2
                rows = slice(e_ * 64, (e_ + 1) * 64)
                pden = bank()
                po = bank()
                for i, (pt, v_ap) in enumerate(pts):
                    mm(pden.ap[:, :], ones_b.ap[:], pt.ap[:], i == 0, i == len(pts) - 1, [ones_b, pt], [pden])
                for i, (pt, v_ap) in enumerate(pts):
                    mm(po.ap[:, :], v_ap, pt.ap[:], i == 0, i == len(pts) - 1, [V, pt], [po])
                den = dens[dcn[0] % 2]
                dcn[0] += 1
                tt("dve", den.ap[rows, :], pden.ap[rows, :],
                   sinkE.ap[rows, kv * 4:(kv + 1) * 4, :].rearrange("p a b -> p (a b)"), ALU.add, [pden, sinkE], [den])
                P.op("dve", lambda e, den=den, rows=rows: e.reciprocal(out=den.ap[rows, :], in_=den.ap[rows, :]),
                     [den.r], [den.r])
                tt("dve", oT.ap[rows, p_ * 4:(p_ + 1) * 4, b * 128:(b + 1) * 128],
                   po.ap[rows, :].rearrange("p (a b) -> p a b", a=4),
                   den.ap[rows, :].rearrange("p (a b) -> p a b", a=4), ALU.mult, [po, den], [oT])

            pend = []
            for b in range(nb):
                blk = b0 + b
                has_prev = blk > 0
                for kv in range(4):
                    p_, e_ = kv // 2, kv % 2
                    rows = slice(e_ * 64, (e_ + 1) * 64)
                    q_ap = qT.ap[rows, p_ * 4:(p_ + 1) * 4, b * 128:(b + 1) * 128]
                    srcs = [("cur", kT.ap[rows, p_, 128 + b * 128:128 + (b + 1) * 128], Ecur, V.ap[:, 1 + b, p_ * 128:(p_ + 1) * 128])]
                    if has_prev:
                        srcs.append(("prev", kT.ap[rows, p_, b * 128:(b + 1) * 128], Eprev, V.ap[:, b, p_ * 128:(p_ + 1) * 128]))
                    pts = []
                    for (nm, k_ap, Etab, v_ap) in srcs:
                        pb = bank()
                        mm(pb.ap[:, :], k_ap, q_ap, True, True, [kT, qT], [pb])
                        eb = ebufs[ec % 4]
                        pt = pTs[ec % 6]
                        ec += 1
                        act(eb.ap[:], pb.ap[:, :], AF.Exp, [pb], [eb], scale=0.125)
                        tt("pool", pt.ap[:], eb.ap[:], Etab.ap[:, kv * 512:(kv + 1) * 512], ALU.mult, [eb, Etab], [pt])
                        pts.append((pt, v_ap))
                    pend.append((b, kv, pts))
                    if len(pend) > 1:
                        attn_back(*pend.pop(0))
            while pend:
                attn_back(*pend.pop(0))
            out_proj_pipelined(ti, tiles, xbs, xn, hTs, l, 0,
                               lambda b, xt=xt, b0=b0: out_proj_block(xt, b, b0 + b, oT, KC, wo, brow))
            cp("pool", kT.ap[:, :, 0:128], kT.ap[:, :, W:W + 128], [kT], [kT])
            cp("pool", V.ap[:, 0, :], V.ap[:, nb, :], [V], [V])
            store_x(xt, b0, nb, dst)
        state["src"] = dst

    def conv_pass(l, dst):
        pass_begin()
        NB = 2
        WMAX = NB * 128
        w_in = A.alloc("cw_in", [128, KC, 2048], BF16)
        w_out = A.alloc("cw_out", [128, KC, D], BF16)
        diag = A.alloc("cdiag", [128, KC * 31, 128], BF16)
        bin_c = A.alloc("cbin", [128, 16], F32)
        dwc = A.alloc("cdwc", [128, 31, KC], F32)
        dwb = A.alloc("cdwb", [128, KC], F32)
        lng = A.alloc("clng", [128, KC], F32)
        lnb = A.alloc("clnb", [128, KC], F32)
        brow = A.alloc("cbo_rb", [128, D], BF16)
        mark = A.off
        gbc = load_gbc(l * 2 + 0)
        dma("pool", w_in.ap[:], cw_in_d, [], [w_in], "wld")
        for (bf, d_) in ((bin_c, cb_in_col_d), (dwc, cdw_col_d), (dwb, cdwb_col_d), (lng, clng_col_d), (lnb, clnb_col_d)):
            dma("sp", bf.ap[:], d_, [], [bf], "ld")
        for c in range(KC):
            for t in range(31):
                ts("pool", diag.ap[:, c * 31 + t, :], ident_f.ap[:], dwc.ap[:, t, c:c + 1], None, ALU.mult, None,
                   [ident_f, dwc], [diag])
        load_w_scaled(w_out, cw_out_d, KC, gbc, "cwo")
        bias_row_scaled(cb_out_d, gbc, "cbo", brow)
        end_setup(mark)
        xbs = [A.alloc("xb%d" % i, [128, NB, D], F32) for i in range(2)]
        xn = A.alloc("xn", [128, NB, D], BF16)
        hTs = [A.alloc("hT%d" % i, [128, KC, WMAX], BF16) for i in range(2)]
        zb = A.alloc("zb", [128, KC, 30 + WMAX], BF16)
        yb = A.alloc("yb", [128, KC, WMAX], F32)
        ybr = [Res("yb%d" % i) for i in range(KC)]
        ybf = [A.alloc("ybf%d" % i, [128, WMAX], BF16) for i in range(3)]
        ysq = [A.alloc("ysq%d" % i, [128, WMAX], BF16) for i in range(3)]
        tmp8 = [A.alloc("tmp8_%d" % i, [128, WMAX], F32) for i in range(KC)]
        sg8 = [A.alloc("sg8_%d" % i, [128, WMAX], F32) for i in range(KC)]
        sg = [A.alloc("sg%d" % i, [128, WMAX], F32) for i in range(2)]
        mean = A.alloc("mean", [128, WMAX], F32)
        rs = A.alloc("rs", [128, WMAX], F32)
        tmp = [A.alloc("tmp%d" % i, [128, WMAX], F32) for i in range(2)]
        sT = A.alloc("sT", [128, KC, WMAX], BF16)
        sTr = [Res("sT%d" % i) for i in range(KC)]
        mset("pool", zb.ap[:], 0.0, [zb])
        tiles = tiles_of(NBLK, NB)
        load_x(xbs[0], *tiles[0])
        for ti, (b0, nb) in enumerate(tiles):
            W = nb * 128
            xt = xbs[ti % 2]
            hT = hTs[ti % 2]
            if ti + 1 < len(tiles):
                load_x(xbs[(ti + 1) % 2], *tiles[ti + 1])
            if ti == 0:
                prologue(xt, xn, hT, b0, nb, l, 0)
            for c in range(KC):
                pa = bank()
                pg = bank()
                for k in range(KC):
                    mm(pa.ap[:, 0:W], w_in.ap[:, k, c * 128:(c + 1) * 128], hT.ap[:, k, 0:W], k == 0, k == KC - 1, [w_in, hT], [pa])
                for k in range(KC):
                    mm(pg.ap[:, 0:W], w_in.ap[:, k, D + c * 128:D + (c + 1) * 128], hT.ap[:, k, 0:W], k == 0, k == KC - 1,
                       [w_in, hT], [pg])
                s_ = sg[c % 2]
                act(s_.ap[:, 0:W], pg.ap[:, 0:W], AF.Sigmoid, [pg, bin_c], [s_], bias=bin_c.ap[:, 8 + c:9 + c])
                stt("dve", zb.ap[:, c, 30:30 + W], pa.ap[:, 0:W], bin_c.ap[:, c:c + 1], s_.ap[:, 0:W], ALU.add, ALU.mult,
                    [pa, bin_c, s_], [zb])
            pmean = bank()
            pmsq = bank()
            bank_excl.extend([pmean, pmsq])
            pend_stats = []

            def conv_stats(c_, yb16_, ys16_, W=W):
                mm(pmean.ap[:, 0:W], onesN_b.ap[:], yb16_.ap[:, 0:W], c_ == 0, c_ == KC - 1, [onesN_b, yb16_], [pmean])
                mm(pmsq.ap[:, 0:W], onesN_b.ap[:], ys16_.ap[:, 0:W], c_ == 0, c_ == KC - 1, [onesN_b, ys16_], [pmsq])

            for c in range(KC):
                py = bank()
                for t in range(31):
                    mm(py.ap[:, 0:W], diag.ap[:, c * 31 + t, :], zb.ap[:, c, t:t + W], t == 0, t == 30, [diag, zb], [py])
                act(yb.ap[:, c, 0:W], py.ap[:, 0:W], AF.Identity, [py, dwb], [ybr[c]], bias=dwb.ap[:, c:c + 1])
                yb16 = ybf[c % 3]
                ys16 = ysq[c % 3]
                act(ys16.ap[:, 0:W], py.ap[:, 0:W], AF.Square, [py, dwb], [ys16], bias=dwb.ap[:, c:c + 1])
                cp("pool", yb16.ap[:, 0:W], yb.ap[:, c, 0:W], [ybr[c]], [yb16])
                pend_stats.append((c, yb16, ys16))
                if len(pend_stats) > 1:
                    conv_stats(*pend_stats.pop(0))
            while pend_stats:
                conv_stats(*pend_stats.pop(0))
            del bank_excl[:]
            cp("dve", mean.ap[:, 0:W], pmean.ap[:, 0:W], [pmean], [mean])
            tt("dve", rs.ap[:, 0:W], mean.ap[:, 0:W], mean.ap[:, 0:W], ALU.mult, [mean], [rs])
            tt("dve", rs.ap[:, 0:W], pmsq.ap[:, 0:W], rs.ap[:, 0:W], ALU.subtract, [pmsq, rs], [rs])
            ts("dve", rs.ap[:, 0:W], rs.ap[:, 0:W], EPS, None, ALU.add, None, [rs], [rs])
            act(rs.ap[:, 0:W], rs.ap[:, 0:W], AF.Sqrt, [rs], [rs])
            P.op("dve", lambda e, W=W: e.reciprocal(out=rs.ap[:, 0:W], in_=rs.ap[:, 0:W]), [rs.r], [rs.r])
            for c in range(KC):
                tt("dve", tmp8[c].ap[:, 0:W], yb.ap[:, c, 0:W], mean.ap[:, 0:W], ALU.subtract, [ybr[c], mean], [tmp8[c]])
            for c in range(KC):
                tt("pool", tmp8[c].ap[:, 0:W], tmp8[c].ap[:, 0:W], rs.ap[:, 0:W], ALU.mult, [tmp8[c], rs], [tmp8[c]])
            for c in range(KC):
                act(sg8[c].ap[:, 0:W], tmp8[c].ap[:, 0:W], AF.Sigmoid, [tmp8[c], lng, lnb], [sg8[c]],
                    scale=lng.ap[:, c:c + 1], bias=lnb.ap[:, c:c + 1])
            for c in range(KC):
                ts("dve", tmp8[c].ap[:, 0:W], tmp8[c].ap[:, 0:W], lng.ap[:, c:c + 1], lnb.ap[:, c:c + 1], ALU.mult, ALU.add,
                   [tmp8[c], lng, lnb], [tmp8[c]])
            for c in range(KC):
                tt("pool", sT.ap[:, c, 0:W], tmp8[c].ap[:, 0:W], sg8[c].ap[:, 0:W], ALU.mult, [tmp8[c], sg8[c]], [sTr[c]])
            out_proj_pipelined(ti, tiles, xbs, xn, hTs, l, 0,
                               lambda b, xt=xt, b0=b0: out_proj_block(xt, b, b0 + b, sT, KC, w_out, brow, sTr))
            cp("pool", zb.ap[:, :, 0:30], zb.ap[:, :, W:W + 30], [zb], [zb])
            store_x(xt, b0, nb, dst)
        state["src"] = dst

    def sgu_pass(l, dst):
        pass_begin()
        NB = 2
        WMAX = NB * 128
        w_in = A.alloc("sw_in", [128, KC, 4096], BF16)
        w_out = A.alloc("sw_out", [128, 16, D], BF16)
        buc = A.alloc("sbuc", [128, 16], F32)
        bvr = A.alloc("sbvr", [128, 2048], BF16)
        lngb = A.alloc("slngb", [128, 2048], F32)
        lnbb = A.alloc("slnbb", [128, 2048], F32)
        wmT = A.alloc("swmT", [128, 8, 128], BF16)
        bsr = A.alloc("sbsr", [128, 1024], BF16)
        brow = A.alloc("sbo_rb", [128, D], BF16)
        mark = A.off
        wm32 = A.alloc("swm32", [128, 8, 128], F32)
        tri = A.alloc("stri", [128, 128], F32)
        gbc = load_gbc(l * 2 + 0)
        dma("pool", w_in.ap[:], sw_in_d, [], [w_in], "wld")
        dma("sp", buc.ap[:], sb_in_ucol_d, [], [buc], "ld")
        dma("pool", bvr.ap[0:1, :], sb_in_v_d, [], [bvr], "wld")
        dma("sp", lngb.ap[:], slng_d.partition_broadcast(128), [], [lngb], "ld")
        dma("sp", lnbb.ap[:], slnb_d.partition_broadcast(128), [], [lnbb], "ld")
        dma("sp", wm32.ap[:], swsT_d, [], [wm32], "ld")
        dma("sp", tri.ap[:], tri_d, [], [tri], "ld")
        for g in range(8):
            tt("dve", wmT.ap[:, g, :], wm32.ap[:, g, :], tri.ap[:], ALU.mult, [wm32, tri], [wmT])
        dma("pool", bsr.ap[0:1, :], sbs_d, [], [bsr], "wld")
        load_w_scaled(w_out, sw_out_d, 16, gbc, "swo")
        bias_row_scaled(sb_out_d, gbc, "sbo", brow)
        end_setup(mark)
        xbs = [A.alloc("xb%d" % i, [128, NB, D], F32) for i in range(2)]
        xn = A.alloc("xn", [128, NB, D], BF16)
        hTs = [A.alloc("hT%d" % i, [128, KC, WMAX], BF16) for i in range(2)]
        uT = A.alloc("uT", [128, 16, WMAX], F32)
        vraw = A.alloc("vraw", [128, 2048], F32)
        vrq = [Res("vraw%d" % i) for i in range(4)]
        vln = A.alloc("vln", [128, 2048], BF16)
        st6 = A.alloc("st6", [128, 4, 6], F32)
        mv = A.alloc("mv", [128, 2], F32)
        rsd = A.alloc("rsd", [128, 1], F32)
        mT = A.alloc("mT", [128, 16, WMAX], BF16)
        tiles = tiles_of(NBLK, NB)
        load_x(xbs[0], *tiles[0])
        for ti, (b0, nb) in enumerate(tiles):
            W = nb * 128
            xt = xbs[ti % 2]
            hT = hTs[ti % 2]
            if ti + 1 < len(tiles):
                load_x(xbs[(ti + 1) % 2], *tiles[ti + 1])
            if ti == 0:
                prologue(xt, xn, hT, b0, nb, l, 0)
            def u_proj(W=W, hT=hT):
                for c in range(16):
                    pb = bank()
                    for k in range(KC):
                        mm(pb.ap[:, 0:W], w_in.ap[:, k, c * 128:(c + 1) * 128], hT.ap[:, k, 0:W], k == 0, k == KC - 1, [w_in, hT], [pb])
                    act(uT.ap[:, c, 0:W], pb.ap[:, 0:W], AF.Gelu, [pb, buc], [uT], bias=buc.ap[:, c:c + 1])

            def v_proj_ln(b, hT=hT):
                pbs = [bank() for _ in range(4)]
                for q in range(4):
                    mm(pbs[q].ap[:, :], ones_b.ap[0:1, :], bvr.ap[0:1, q * 512:(q + 1) * 512], True, False, [ones_b, bvr], [pbs[q]])
                for k in range(KC):
                    for q in range(4):
                        mm(pbs[q].ap[:, :], hT.ap[:, k, b * 128:(b + 1) * 128], w_in.ap[:, k, 2048 + q * 512:2048 + (q + 1) * 512],
                           False, k == KC - 1, [hT, w_in], [pbs[q]])
                for q in range(4):
                    act(vraw.ap[:, q * 512:(q + 1) * 512], pbs[q].ap[:, :], AF.Gelu, [pbs[q]], [vrq[q]])
                for q in range(4):
                    P.op("dve", lambda e, q=q: e.bn_stats(out=st6.ap[:, q, :], in_=vraw.ap[:, q * 512:(q + 1) * 512]),
                         [vrq[q]], [st6.r])
                P.op("dve", lambda e: e.bn_aggr(out=mv.ap[:, :], in_=st6.ap[:, :, :]), [st6.r], [mv.r])
                ts("dve", rsd.ap[:], mv.ap[:, 1:2], EPS, None, ALU.add, None, [mv], [rsd])
                act(rsd.ap[:], rsd.ap[:], AF.Sqrt, [rsd], [rsd])
                P.op("dve", lambda e: e.reciprocal(out=rsd.ap[:], in_=rsd.ap[:]), [rsd.r], [rsd.r])
                ts("dve", vraw.ap[:], vraw.ap[:], mv.ap[:, 0:1], rsd.ap[:, 0:1], ALU.subtract, ALU.mult, vrq + [mv, rsd], vrq)
                tt("pool", vraw.ap[:], vraw.ap[:], lngb.ap[:], ALU.mult, vrq + [lngb], vrq)
                tt("pool", vln.ap[:], vraw.ap[:], lnbb.ap[:], ALU.add, vrq + [lnbb], [vln])

            def gating(b):
                for q in range(4):
                    pb = bank()
                    for cc in range(4):
                        fc = q * 4 + cc
                        g = fc // 2
                        mm(pb.ap[:, cc * 128:(cc + 1) * 128], ones_b.ap[0:1, :], bsr.ap[0:1, g * 128:(g + 1) * 128], True, False,
                           [ones_b, bsr], [pb])
                        mm(pb.ap[:, cc * 128:(cc + 1) * 128], vln.ap[:, fc * 128:(fc + 1) * 128], wmT.ap[:, g, :], False, True,
                           [vln, wmT], [pb])
                    tt("dve", mT.ap[:, q * 4:(q + 1) * 4, b * 128:(b + 1) * 128], pb.ap[:, :].rearrange("p (a b) -> p a b", a=4),
                       uT.ap[:, q * 4:(q + 1) * 4, b * 128:(b + 1) * 128], ALU.mult, [pb, uT], [mT])

            nxt = tiles[ti + 1] if ti + 1 < len(tiles) else None
            v_proj_ln(0)
            u_proj()
            gating(0)
            for b in range(1, nb):
                v_proj_ln(b)
                if nxt is not None and b == 1:
                    prologue_a(xbs[(ti + 1) % 2], xn, nxt[0], nxt[1])
                out_proj_block(xt, b - 1, b0 + b - 1, mT, 16, w_out, brow)
                gating(b)
            if nxt is not None:
                if nb == 1:
                    prologue_a(xbs[(ti + 1) % 2], xn, nxt[0], nxt[1])
                prologue_b(xn, hTs[(ti + 1) % 2], nxt[1], l, 0)
            out_proj_block(xt, nb - 1, b0 + nb - 1, mT, 16, w_out, brow)
            store_x(xt, b0, nb, dst)
        state["src"] = dst

    def final_pass():
        pass_begin()
        NB = 2
        fg = A.alloc("fg", [128, D], F32)
        dma("sp", fg.ap[:], final_g.partition_broadcast(128), [], [fg], "ld")
        xbs = [A.alloc("xb%d" % i, [128, NB, D], F32) for i in range(2)]
        tiles = tiles_of(NBLK, NB)
        load_x(xbs[0], *tiles[0])
        for ti, (b0, nb) in enumerate(tiles):
            xt = xbs[ti % 2]
            if ti + 1 < len(tiles):
                load_x(xbs[(ti + 1) % 2], *tiles[ti + 1])
            for b in range(nb):
                act(xt.ap[:, b, :], xt.ap[:, b, :], AF.Copy, [xt, rstd], [xt], scale=rstd.ap[:, b0 + b:b0 + b + 1])
                tt("dve" if b % 2 == 0 else "pool", xt.ap[:, b, :], xt.ap[:, b, :], fg.ap[:], ALU.mult, [xt, fg], [xt])
            store_x(xt, b0, nb, out_d)

    for pname in passes:
        kind, l = pname[0], int(pname[1])
        if kind == "a":
            attn_pass(l, l // 3, xs_d)
        elif kind == "c":
            conv_pass(l, xs_d)
        elif kind == "s":
            sgu_pass(l, xs_d)
        elif kind == "f":
            ffn_pass(l, xs_d)
    if debug_out:
        P.barrier()
        A.reset()
        xb = A.alloc("xdbg", [128, 2, D], F32)
        for (b0, nb) in tiles_of(NBLK, 2):
            load_x(xb, b0, nb)
            store_x(xb, b0, nb, out_d)
    else:
        final_pass()
    _LAST_PROG[0] = P
    P.emit()
    es.close()
    return nc


def _cols(v, n):
    return np.ascontiguousarray(np.asarray(v, np.float32).reshape(n, 128).T)


def _wk(w):
    w = np.asarray(w, np.float32)
    K, N = w.shape
    return np.ascontiguousarray(w.reshape(K // 128, 128, N).transpose(1, 0, 2))


def _alibi_tables():
    slopes = (2.0 ** (-8.0 * np.arange(1, 17, dtype=np.float64) / 16)).reshape(4, 4)
    s = np.arange(128)[:, None]
    q = np.arange(128)[None, :]
    dist_cur = (q - s).astype(np.float64)
    dist_prev = (q + 128 - s).astype(np.float64)
    Ec = np.zeros((128, 4, 4, 128), np.float64)
    Ep = np.zeros((128, 4, 4, 128), np.float64)
    for kv in range(4):
        for g in range(4):
            Ec[:, kv, g, :] = np.where((dist_cur >= 0) & (dist_cur < 128), np.exp(-slopes[kv, g] * dist_cur), 0.0)
            Ep[:, kv, g, :] = np.where((dist_prev >= 0) & (dist_prev < 128), np.exp(-slopes[kv, g] * dist_prev), 0.0)
    return Ec.reshape(128, 2048).astype(np.float32), Ep.reshape(128, 2048).astype(np.float32)


def prep_shared(inp):
    f = lambda a: np.asarray(a, np.float32)
    sh = {}
    aw = f(inp["ada_w"]).reshape(NL, KC, 128, 6, D)
    sh["adaw"] = np.ascontiguousarray(aw.transpose(0, 3, 2, 1, 4))
    ab = f(inp["ada_b"]).reshape(NL, 6, KC, 128)
    sh["adab_col"] = np.ascontiguousarray(ab.transpose(3, 0, 1, 2))
    sh["adab_row"] = np.ascontiguousarray(f(inp["ada_b"]).reshape(NL, 6, D))
    ng = np.stack([f(inp["norm1_g"]), f(inp["norm2_g"])], 1).reshape(NL, 2, KC, 128)
    sh["ng_col"] = np.ascontiguousarray(ng.transpose(3, 0, 1, 2))
    sh["final_g"] = f(inp["final_g"]).reshape(1, D)
    sh["ident"] = np.eye(128, dtype=np.float32)
    sidx = np.arange(128)[:, None]
    tidx = np.arange(128)[None, :]
    sh["tri"] = (tidx >= sidx).astype(np.float32)
    sh["E_cur"], sh["E_prev"] = _alibi_tables()
    heads = []
    for p in range(2):
        for g in range(4):
            heads += [4 * (2 * p) + g, 4 * (2 * p + 1) + g]
    qcols = np.concatenate([np.arange(h * 64, (h + 1) * 64) for h in heads])
    wqkv = f(inp["attn_wqkv"])
    bqkv = f(inp["attn_bqkv"])
    wqk = np.concatenate([wqkv[:, :, qcols], wqkv[:, :, 1024:1280]], axis=2)
    sh["wqk"] = np.stack([_wk(wqk[j]) for j in range(2)])
    sh["wv"] = np.stack([_wk(wqkv[j][:, 1280:1536]) for j in range(2)])
    bqk = np.concatenate([bqkv[:, qcols], bqkv[:, 1024:1280]], axis=1)
    sh["bqk_col"] = np.ascontiguousarray(np.stack([_cols(bqk[j], 10) for j in range(2)], 1))
    sh["bv"] = np.ascontiguousarray(bqkv[:, 1280:1536].reshape(2, 1, 256))
    wo = f(inp["attn_wo"])
    sh["wo"] = np.stack([_wk(wo[j][qcols, :]) for j in range(2)])
    sh["bo"] = f(inp["attn_bo"]).reshape(2, 1, D)
    sh["sinks"] = f(inp["attn_sinks"]).reshape(2, 1, 16)
    sh["cw_in"] = _wk(f(inp["conv_w_in"])[0])
    sh["cb_in_col"] = _cols(f(inp["conv_b_in"])[0], 16)
    cdw = f(inp["conv_dw"])[0].reshape(31, KC, 128)
    sh["cdw_col"] = np.ascontiguousarray(cdw.transpose(2, 0, 1))
    sh["cdwb_col"] = _cols(f(inp["conv_dw_b"])[0], KC)
    sh["clng_col"] = _cols(f(inp["conv_ln_g"])[0], KC)
    sh["clnb_col"] = _cols(f(inp["conv_ln_b"])[0], KC)
    sh["cw_out"] = _wk(f(inp["conv_w_out"])[0])
    sh["cb_out"] = f(inp["conv_b_out"]).reshape(1, D)
    sh["sw_in"] = _wk(f(inp["sgu_w_in"])[0])
    sbin = f(inp["sgu_b_in"])[0]
    sh["sb_in_ucol"] = _cols(sbin[:2048], 16)
    sh["sb_in_v"] = np.ascontiguousarray(sbin[2048:].reshape(1, 2048))
    sh["slng"] = f(inp["sgu_ln_g"]).reshape(1, 2048)
    sh["slnb"] = f(inp["sgu_ln_b"]).reshape(1, 2048)
    sh["swsT"] = np.ascontiguousarray(f(inp["sgu_ws"])[0].transpose(2, 0, 1))
    sh["sbs"] = np.ascontiguousarray(f(inp["sgu_bs"])[0].reshape(1, 1024))
    sh["sw_out"] = _wk(f(inp["sgu_w_out"])[0])
    sh["sb_out"] = f(inp["sgu_b_out"]).reshape(1, D)
    fwi = f(inp["ffn_w_in"])
    sh["fw_in"] = np.stack([_wk(fwi[l]) for l in range(NL)])
    fdw = f(inp["ffn_dw"]).reshape(NL, 3, 44, 128)
    sh["fdw_col"] = np.ascontiguousarray(fdw.transpose(3, 0, 1, 2))
    fdwb = f(inp["ffn_dw_b"]).reshape(NL, 44, 128)
    sh["fdwb_col"] = np.ascontiguousarray(fdwb.transpose(2, 0, 1))
    fwo = f(inp["ffn_w_out"])
    sh["fw_out"] = np.stack([_wk(fwo[l]) for l in range(NL)])
    return sh


_NC_CACHE = {}
_LAST_PROG = [None]


def kernel(**inputs):
    x = np.asarray(inputs["x"], np.float32)
    c = np.asarray(inputs["c"], np.float32)
    B = x.shape[0]
    shared = prep_shared(inputs)
    NBLK = NBLK_FULL
    T = NBLK * 128
    if "nc" not in _NC_CACHE:
        _NC_CACHE["nc"] = build_program(NBLK)
    nc = _NC_CACHE["nc"]
    in_maps = []
    for core in range(8):
        b, half = core // 2, core % 2
        t0 = 0 if half == 0 else SEQ - T
        m = dict(shared)
        m["x"] = np.ascontiguousarray(x[b, t0:t0 + T, :])
        m["ccol"] = _cols(c[b], KC)
        in_maps.append(m)
    res = run_bass_kernel_spmd(nc, in_maps, core_ids=list(range(8)))
    out = np.empty((B, SEQ, D), np.float32)
    for core in range(8):
        b, half = core // 2, core % 2
        o = np.asarray(res.results[core]["out"]).reshape(T, D)
        if half == 0:
            out[b, 0:T] = o
        else:
            out[b, T:SEQ] = o[2 * T - SEQ:]
    return out
```

```python
import contextlib
import numpy as np
import concourse.bass as bass
import concourse.mybir as mybir
from concourse.bass_utils import run_bass_kernel_spmd

F32 = mybir.dt.float32
BF16 = mybir.dt.bfloat16
AF = mybir.ActivationFunctionType
ALU = mybir.AluOpType

D = 1024
KC = 8
NL = 4
FFN = 2816
FJ = 22
NBLK_FULL = 34
SEQ = 8192
EPS = 1e-6
COMPUTE = ("pe", "act", "dve", "pool")


class Res:
    __slots__ = ("name", "last_write", "reads", "uid", "excl")
    _n = [0]

    def __init__(self, name=""):
        self.name = name
        self.excl = False
        self.last_write = None
        self.reads = []
        Res._n[0] += 1
        self.uid = Res._n[0]


class Op:
    __slots__ = ("eng", "fn", "deps", "signal", "sem", "sigval", "stream")

    def __init__(self, eng, fn, stream=None):
        self.eng = eng
        self.fn = fn
        self.deps = []
        self.signal = False
        self.sem = None
        self.sigval = None
        self.stream = stream


class Prog:
    def __init__(self, nc):
        self.nc = nc
        self.ops = {e: [] for e in COMPUTE + ("sp",)}
        self.phase = 0
        self.pending = {e: [] for e in COMPUTE + ("sp",)}
        self.last_stream = {}
        self.dma_slots = {}
        self.all_ops = []

    def barrier(self, new_phase=True):
        lasts = []
        for e, lst in self.ops.items():
            for o in reversed(lst):
                if o.stream is None:
                    lasts.append(o)
                    break
        lasts += list(self.last_stream.values())
        for e in self.pending:
            self.pending[e] = list(lasts)
        if new_phase:
            self.phase += 1
            self.dma_slots = {}

    def op(self, eng, fn, reads=(), writes=(), stream=None):
        if stream is not None:
            stream = self.dma_slots.setdefault(stream.uid, len(self.dma_slots))
        o = Op(eng, fn, stream)
        o.sem = ("dma", stream) if stream is not None else (eng, self.phase)
        if stream is not None:
            o.signal = True
            self.last_stream[stream] = o
        deps = []
        for r in reads:
            if r.last_write is not None:
                deps.append((r.last_write, "raw"))
            if r.excl:
                for rd in r.reads:
                    if rd.eng != eng:
                        deps.append((rd, "rar"))
        for w in writes:
            if w.last_write is not None:
                deps.append((w.last_write, "waw"))
            for rd in w.reads:
                deps.append((rd, "war"))
        for d in self.pending[eng]:
            deps.append((d, "bar"))
        self.pending[eng] = []
        seen = set()
        for d, kind in deps:
            if d is o or id(d) in seen:
                continue
            same = (d.eng == eng) and d.stream is None and stream is None
            if same:
                if eng == "pe":
                    continue
                if kind not in ("raw",):
                    continue
            seen.add(id(d))
            d.signal = True
            o.deps.append(d)
        for r in reads:
            r.reads.append(o)
        for w in writes:
            w.last_write = o
            w.reads = []
        self.ops[eng].append(o)
        self.all_ops.append(o)
        return o

    def emit(self):
        nc = self.nc
        counts = {}
        for o in self.all_ops:
            if o.signal:
                inc = 16 if o.stream is not None else 1
                counts[o.sem] = counts.get(o.sem, 0) + inc
                o.sigval = counts[o.sem]
        sem_keys = sorted(counts.keys(), key=str)
        with contextlib.ExitStack() as es:
            sems = {}
            for k in sem_keys:
                sems[k] = es.enter_context(nc.semaphore("s_%s_%s" % (k[0], k[1])))
            block = es.enter_context(nc.Block())

            def run(ename):
                def _f(eng):
                    waited = {}
                    for o in self.ops[ename]:
                        need = {}
                        for d in o.deps:
                            if need.get(d.sem, 0) < d.sigval:
                                need[d.sem] = d.sigval
                        for sk_, v_ in need.items():
                            if waited.get(sk_, 0) < v_:
                                eng.wait_ge(sems[sk_], v_)
                                waited[sk_] = v_
                        ins = o.fn(eng)
                        if o.signal:
                            ins.then_inc(sems[o.sem], 16 if o.stream is not None else 1)
                    if ename == "sp":
                        for k in sem_keys:
                            if k[0] == "dma":
                                eng.wait_ge(sems[k], counts[k])
                return _f

            block.sync(run("sp"))
            block.tensor(run("pe"))
            block.scalar(run("act"))
            block.vector(run("dve"))
            block.gpsimd(run("pool"))


class Buf:
    __slots__ = ("ap", "r")

    def __init__(self, ap, name=""):
        self.ap = ap
        self.r = Res(name)


class Arena:
    def __init__(self, base_ap, words):
        self.base = base_ap
        self.words = words
        self.off = 0

    def reset(self):
        self.off = 0

    def alloc(self, name, shape, dt):
        assert shape[0] == 128
        n = 1
        for s in shape[1:]:
            n *= s
        nbytes = n * (2 if dt == BF16 else 4)
        words = (nbytes + 3) // 4
        words = (words + 7) // 8 * 8
        assert self.off + words <= self.words, "arena overflow %s: %d + %d > %d" % (name, self.off, words, self.words)
        v = self.base[:, self.off:self.off + words]
        self.off += words
        if dt == BF16:
            v = v.bitcast(BF16)
        v = v[:, 0:n]
        if len(shape) == 3:
            v = v.rearrange("p (a b) -> p a b", a=shape[1])
        elif len(shape) == 4:
            v = v.rearrange("p (a b c) -> p a b c", a=shape[1], b=shape[2])
        return Buf(v, name)


def tiles_of(nblk, nb):
    t = []
    b = 0
    while b < nblk:
        n = min(nb, nblk - b)
        t.append((b, n))
        b += n
    return t


def build_program(NBLK=NBLK_FULL, passes=None, debug_out=False):
    if passes is None:
        passes = ["a0", "f0", "c1", "f1", "s2", "f2", "a3", "f3"]
    nc = bass.Bass("TRN2", target_bir_lowering=False)
    T = NBLK * 128

    def din(name, shape):
        return nc.dram_tensor(name, list(shape), F32, kind="ExternalInput").ap()

    x_in = din("x", [T, D])
    ccol = din("ccol", [128, KC])
    adaw = din("adaw", [NL, 6, 128, KC, D])
    adab_col = din("adab_col", [128, NL, 6, KC])
    adab_row = din("adab_row", [NL, 6, D])
    ng_col = din("ng_col", [128, NL, 2, KC])
    final_g = din("final_g", [1, D])
    ident_d = din("ident", [128, 128])
    tri_d = din("tri", [128, 128])
    E_cur_d = din("E_cur", [128, 2048])
    E_prev_d = din("E_prev", [128, 2048])
    wqk_d = din("wqk", [2, 128, KC, 1280])
    wv_d = din("wv", [2, 128, KC, 256])
    bqk_col_d = din("bqk_col", [128, 2, 10])
    bv_d = din("bv", [2, 1, 256])
    wo_d = din("wo", [2, 128, KC, D])
    bo_d = din("bo", [2, 1, D])
    sinks_d = din("sinks", [2, 1, 16])
    cw_in_d = din("cw_in", [128, KC, 2048])
    cb_in_col_d = din("cb_in_col", [128, 16])
    cdw_col_d = din("cdw_col", [128, 31, KC])
    cdwb_col_d = din("cdwb_col", [128, KC])
    clng_col_d = din("clng_col", [128, KC])
    clnb_col_d = din("clnb_col", [128, KC])
    cw_out_d = din("cw_out", [128, KC, D])
    cb_out_d = din("cb_out", [1, D])
    sw_in_d = din("sw_in", [128, KC, 4096])
    sb_in_ucol_d = din("sb_in_ucol", [128, 16])
    sb_in_v_d = din("sb_in_v", [1, 2048])
    slng_d = din("slng", [1, 2048])
    slnb_d = din("slnb", [1, 2048])
    swsT_d = din("swsT", [128, 8, 128])
    sbs_d = din("sbs", [1, 8 * 128])
    sw_out_d = din("sw_out", [128, 16, D])
    sb_out_d = din("sb_out", [1, D])
    fw_in_d = din("fw_in", [NL, 128, KC, 2 * FFN])
    fdw_col_d = din("fdw_col", [128, NL, 3, 44])
    fdwb_col_d = din("fdwb_col", [128, NL, 44])
    fw_out_d = din("fw_out", [NL, 128, FJ, D])

    out_d = nc.dram_tensor("out", [T, D], F32, kind="ExternalOutput").ap()
    xs_d = nc.dram_tensor("xs", [T, D], F32).ap()
    grow_d = nc.dram_tensor("grow", [NL * 2, D], F32).ap()

    P = Prog(nc)
    es = contextlib.ExitStack()

    def sb(name, shape, dt):
        return Buf(es.enter_context(nc.sbuf_tensor(name, list(shape), dt)), name)

    AW = 50500
    arena_t = es.enter_context(nc.sbuf_tensor("arena", [128, AW], F32))
    A = Arena(arena_t[:, :], AW)

    ident_f = sb("ident_f", [128, 128], F32)
    ident_b = sb("ident_b", [128, 128], BF16)
    ones_b = sb("ones_b", [128, 128], BF16)
    onesN_b = sb("onesN_b", [128, 128], BF16)
    cact = sb("cact", [128, KC], F32)
    modc = sb("modc", [128, NL, 6, KC], F32)
    gsc = sb("gsc", [128, NL, 2, KC], F32)
    ngc = sb("ngc", [128, NL, 2, KC], F32)
    adabc = sb("adabc", [128, NL, 6, KC], F32)
    ssq = sb("ssq", [128, NBLK], F32)
    rstd = sb("rstd", [128, NBLK], F32)
    junk = sb("junk", [128, D], F32)

    banksT = [Buf(es.enter_context(nc.psum_tensor("pT%d" % i, [128, 1024], BF16)), "pT%d" % i) for i in range(2)]
    banks = [Buf(es.enter_context(nc.psum_tensor("pb%d" % i, [128, 512], F32)), "pb%d" % i) for i in range(6)]
    for b_ in banksT + banks:
        b_.r.excl = True
    bank_ctr = [0]
    bankT_ctr = [0]

    bank_excl = []

    def bank():
        while True:
            b = banks[bank_ctr[0] % 6]
            bank_ctr[0] += 1
            if not any(b is x for x in bank_excl):
                return b

    def end_setup(mark):
        P.barrier(new_phase=False)
        A.off = mark

    def bankT():
        b = banksT[bankT_ctr[0] % 2]
        bankT_ctr[0] += 1
        return b

    def R(bufs):
        return [b.r if isinstance(b, Buf) else b for b in bufs]

    def dma(eng, out, in_, reads, writes, stream=None):
        key = R(writes)[0] if writes else R(reads)[0]
        P.op(eng, lambda e: e.dma_start(out=out, in_=in_), R(reads), R(writes), stream=key)

    def mm(out, lhsT, rhs, start, stop, reads, writes):
        P.op("pe", lambda e: e.matmul(out, lhsT=lhsT, rhs=rhs, start=start, stop=stop), R(reads), R(writes))

    def tr(out, in_, ident, reads, writes):
        P.op("pe", lambda e: e.transpose(out=out, in_=in_, identity=ident), R(reads), R(writes))

    def act(out, in_, func, reads, writes, **kw):
        P.op("act", lambda e: e.activation(out=out, in_=in_, func=func, **kw), R(reads), R(writes))

    def tt(eng, out, in0, in1, op, reads, writes):
        P.op(eng, lambda e: e.tensor_tensor(out=out, in0=in0, in1=in1, op=op), R(reads), R(writes))

    def ts(eng, out, in0, s1, s2, op0, op1, reads, writes):
        if s2 is None:
            P.op(eng, lambda e: e.tensor_scalar(out=out, in0=in0, scalar1=s1, scalar2=None, op0=op0), R(reads), R(writes))
        else:
            P.op(eng, lambda e: e.tensor_scalar(out=out, in0=in0, scalar1=s1, scalar2=s2, op0=op0, op1=op1),
                 R(reads), R(writes))

    def stt(eng, out, in0, scalar, in1, op0, op1, reads, writes):
        P.op(eng, lambda e: e.scalar_tensor_tensor(out=out, in0=in0, scalar=scalar, in1=in1, op0=op0, op1=op1),
             R(reads), R(writes))

    def cp(eng, out, in_, reads, writes):
        P.op(eng, lambda e: e.tensor_copy(out=out, in_=in_), R(reads), R(writes))

    def mset(eng, ap, val, writes):
        P.op(eng, lambda e: e.memset(ap, val), [], R(writes))

    dma("sp", ident_f.ap[:], ident_d, [], [ident_f], "ld")
    dma("sp", cact.ap[:], ccol, [], [cact], "ld")
    dma("sp", adabc.ap[:], adab_col, [], [adabc], "ld")
    dma("sp", ngc.ap[:], ng_col, [], [ngc], "ld")
    cp("dve", ident_b.ap[:], ident_f.ap[:], [ident_f], [ident_b])
    mset("pool", ones_b.ap[:], 1.0, [ones_b])
    mset("pool", onesN_b.ap[:], 1.0 / 1024.0, [onesN_b])
    mset("pool", ssq.ap[:], 0.0, [ssq])
    act(cact.ap[:], cact.ap[:], AF.Silu, [cact], [cact])

    slabs = [A.alloc("slab%d" % i, [128, KC, D], F32) for i in range(2)]
    arow = A.alloc("arow", [128, D], F32)
    grow_sb = A.alloc("grow_sb", [128, D], F32)
    mrow_d = nc.dram_tensor("mrow", [16, D], F32).ap()
    mrow_res = Res("mrow_d")
    si = 0
    for l in range(NL):
        for j in range(6):
            slab = slabs[si % 2]
            si += 1
            dma("sp", slab.ap[:], adaw[l, j], [], [slab], "ld")
            dma("sp", arow.ap[0:1, :], adab_row[l, j:j + 1, :], [], [arow], "ld")
            pbs = [bank(), bank()]
            for h in range(2):
                for k in range(KC):
                    mm(pbs[h].ap[0:1, :], cact.ap[:, k:k + 1], slab.ap[:, k, h * 512:(h + 1) * 512],
                       k == 0, k == KC - 1, [slab, cact], [pbs[h]])
            for h in range(2):
                tt("dve", grow_sb.ap[0:1, h * 512:(h + 1) * 512], pbs[h].ap[0:1, :], arow.ap[0:1, h * 512:(h + 1) * 512],
                   ALU.add, [pbs[h], arow], [grow_sb])
            if j in (2, 5):
                gi = l * 2 + (0 if j == 2 else 1)
                dma("sp", grow_d[gi:gi + 1, :], grow_sb.ap[0:1, :], [grow_sb], [], "st")
            else:
                r_ = l * 4 + {0: 0, 1: 1, 3: 2, 4: 3}[j]
                P.op("sp", lambda e, r_=r_: e.dma_start(out=mrow_d[r_:r_ + 1, :], in_=grow_sb.ap[0:1, :]),
                     [grow_sb.r], [mrow_res], stream=grow_sb.r)
    mst = A.alloc("mst", [128, 128], F32)
    P.op("sp", lambda e: e.dma_start(out=mst.ap[:], in_=mrow_d.rearrange("r (c p) -> (r c) p", p=128)),
         [mrow_res], [mst.r], stream=mst.r)
    pbm = bank()
    tr(pbm.ap[:, 0:128], mst.ap[:], ident_f.ap[:], [mst, ident_f], [pbm])
    for l in range(NL):
        for jj, j in enumerate((0, 1, 3, 4)):
            cp("dve", modc.ap[:, l, j, :], pbm.ap[:, (l * 4 + jj) * 8:(l * 4 + jj + 1) * 8], [pbm], [modc])
    for l in range(NL):
        for s in range(2):
            stt("dve", gsc.ap[:, l, s, :], modc.ap[:, l, 1 + 3 * s, :], 1.0, ngc.ap[:, l, s, :], ALU.add, ALU.mult,
                [modc, ngc], [gsc])

    P.barrier()
    A.reset()
    NB = 2
    tiles = tiles_of(NBLK, NB)
    xb0 = [A.alloc("xb%d" % i, [128, NB, D], F32) for i in range(2)]
    for ti, (b0, nb) in enumerate(tiles):
        xt = xb0[ti % 2]
        dma("sp", xt.ap[:, 0:nb, :], x_in[b0 * 128:(b0 + nb) * 128, :].rearrange("(b p) d -> p b d", p=128), [], [xt], "ld")
        for b in range(nb):
            act(junk.ap[:], xt.ap[:, b, :], AF.Square, [xt], [junk, ssq], accum_out=ssq.ap[:, b0 + b:b0 + b + 1])

    state = {"src": x_in}

    def pass_begin():
        P.barrier()
        A.reset()
        ts("dve", rstd.ap[:], ssq.ap[:], 1.0 / D, EPS, ALU.mult, ALU.add, [ssq], [rstd])
        act(rstd.ap[:], rstd.ap[:], AF.Sqrt, [rstd], [rstd])
        P.op("dve", lambda e: e.reciprocal(out=rstd.ap[:], in_=rstd.ap[:]), [rstd.r], [rstd.r])
        mset("pool", ssq.ap[:], 0.0, [ssq])

    def load_x(xt, b0, nb):
        src = state["src"]
        dma("sp", xt.ap[:, 0:nb, :], src[b0 * 128:(b0 + nb) * 128, :].rearrange("(b p) d -> p b d", p=128), [], [xt], "ld")

    def store_x(xt, b0, nb, dst):
        dma("sp", dst[b0 * 128:(b0 + nb) * 128, :].rearrange("(b p) d -> p b d", p=128), xt.ap[:, 0:nb, :], [xt], [], "st")

    def prologue_a(xt, xn, b0, nb):
        for b in range(nb):
            act(xn.ap[:, b, :], xt.ap[:, b, :], AF.Copy, [xt, rstd], [xn], scale=rstd.ap[:, b0 + b:b0 + b + 1])

    def prologue_b(xn, hT, nb, l, s):
        W = nb * 128
        cpb = 1024 // W
        for k0 in range(0, KC, cpb):
            pt = bankT()
            for kk in range(cpb):
                k = k0 + kk
                for b in range(nb):
                    tr(pt.ap[:, kk * W + b * 128:kk * W + (b + 1) * 128], xn.ap[:, b, k * 128:(k + 1) * 128], ident_b.ap[:],
                       [xn, ident_b], [pt])
            for kk in range(cpb):
                k = k0 + kk
                ts("dve", hT.ap[:, k, 0:W], pt.ap[:, kk * W:(kk + 1) * W], gsc.ap[:, l, s, k:k + 1],
                   modc.ap[:, l, 3 * s, k:k + 1], ALU.mult, ALU.add, [pt, gsc, modc], [hT])

    def prologue(xt, xn, hT, b0, nb, l, s):
        prologue_a(xt, xn, b0, nb)
        prologue_b(xn, hT, nb, l, s)

    def out_proj_pipelined(ti, tiles, xbs, xn, hTs, l, s, do_block):
        b0, nb = tiles[ti]
        nxt = tiles[ti + 1] if ti + 1 < len(tiles) else None
        for b in range(nb):
            if nxt is not None:
                if b == 0:
                    prologue_a(xbs[(ti + 1) % 2], xn, nxt[0], nxt[1])
                if b == nb - 1:
                    prologue_b(xn, hTs[(ti + 1) % 2], nxt[1], l, s)
            do_block(b)

    def epilogue_block(xt, b, blk, pbs):
        for h in range(2):
            tt("dve", xt.ap[:, b, h * 512:(h + 1) * 512], pbs[h].ap[:, :], xt.ap[:, b, h * 512:(h + 1) * 512], ALU.add,
               [pbs[h], xt], [xt])
        act(junk.ap[:], xt.ap[:, b, :], AF.Square, [xt], [junk, ssq], accum_out=ssq.ap[:, blk:blk + 1])

    def load_gbc(gi):
        g = A.alloc("gbc", [128, D], F32)
        dma("sp", g.ap[:], grow_d[gi:gi + 1, :].partition_broadcast(128), [], [g], "ld")
        return g

    def load_w_scaled(dst, src_d, nchunks, gbc, name):
        stg = [A.alloc("%s_stg%d" % (name, i), [128, D], F32) for i in range(2)]
        for j in range(nchunks):
            s_ = stg[j % 2]
            dma("sp", s_.ap[:], src_d[:, j, :], [], [s_], "ld")
            tt("pool", dst.ap[:, j, :], s_.ap[:], gbc.ap[:], ALU.mult, [s_, gbc], [dst])

    def bias_row_scaled(b_d, gbc, name, rb):
        r32 = A.alloc(name + "_r32", [128, D], F32)
        dma("sp", r32.ap[0:1, :], b_d, [], [r32], "ld")
        tt("dve", rb.ap[0:1, :], r32.ap[0:1, :], gbc.ap[0:1, :], ALU.mult, [r32, gbc], [rb])
        return rb

    def out_proj_block(xt, b, blk, lhs_buf, nk, w_buf, brow, lhs_res=None):
        pbs = [bank(), bank()]
        for h in range(2):
            first = True
            if brow is not None:
                mm(pbs[h].ap[:, :], ones_b.ap[0:1, :], brow.ap[0:1, h * 512:(h + 1) * 512], True, False,
                   [ones_b, brow], [pbs[h]])
                first = False
            for k in range(nk):
                mm(pbs[h].ap[:, :], lhs_buf.ap[:, k, b * 128:(b + 1) * 128], w_buf.ap[:, k, h * 512:(h + 1) * 512],
                   first and k == 0, k == nk - 1, [lhs_res[k] if lhs_res is not None else lhs_buf, w_buf], [pbs[h]])
        epilogue_block(xt, b, blk, pbs)

    def ffn_pass(l, dst):
        pass_begin()
        NB = 2
        WMAX = NB * 128
        w_in = A.alloc("fw_in", [128, KC, 2 * FFN], BF16)
        w_out = A.alloc("fw_out", [128, FJ, D], BF16)
        dwc = A.alloc("fdwc", [128, 3, 44], F32)
        dwb = A.alloc("fdwb", [128, 44], F32)
        carry = A.alloc("fcarry", [128, FJ, 2, 2], F32)
        cc = A.alloc("fcc", [128, FJ, 2, 2], F32)
        ct1 = A.alloc("fct1", [128, FJ, 2], F32)
        ct2 = A.alloc("fct2", [128, FJ, 2], F32)
        mark = A.off
        gbc = load_gbc(l * 2 + 1)
        wgr = [Res("fw_in_g%d" % i) for i in range(4)]
        wbounds = [0, 6, 12, 17, FJ]
        wgrp = {}
        for gi_ in range(4):
            j0_, j1_ = wbounds[gi_], wbounds[gi_ + 1]
            for j_ in range(j0_, j1_):
                wgrp[j_] = wgr[gi_]
            for part in range(2):
                c0_, c1_ = part * FFN + j0_ * 128, part * FFN + j1_ * 128
                dma("pool", w_in.ap[:, :, c0_:c1_], fw_in_d[l, :, :, c0_:c1_], [], [wgr[gi_]], "wld")
        dma("sp", dwc.ap[:], fdw_col_d[:, l], [], [dwc], "ld")
        dma("sp", dwb.ap[:], fdwb_col_d[:, l], [], [dwb], "ld")
        mset("pool", carry.ap[:], 0.0, [carry])
        load_w_scaled(w_out, fw_out_d[l], FJ, gbc, "fwo")
        end_setup(mark)
        xbs = [A.alloc("xb%d" % i, [128, NB, D], F32) for i in range(2)]
        xn = A.alloc("xn", [128, NB, D], BF16)
        hTs = [A.alloc("hT%d" % i, [128, KC, WMAX], BF16) for i in range(2)]
        aT = A.alloc("aT", [128, FJ, WMAX], BF16)
        aTr = [Res("aT%d" % i) for i in range(FJ)]
        NY = 6
        ybufs = [A.alloc("y%d" % i, [128, 2, WMAX], F32) for i in range(NY)]
        yres = [(Res("yg%d" % i), Res("yu%d" % i)) for i in range(NY)]
        sbufs = [A.alloc("s%d" % i, [128, WMAX], F32) for i in range(3)]
        tubufs = [A.alloc("tu%d" % i, [128, WMAX], F32) for i in range(3)]
        dw0v = dwc.ap[:, 0, :].rearrange("p (a j) -> p j a", a=2)
        dw1v = dwc.ap[:, 1, :].rearrange("p (a j) -> p j a", a=2)
        tiles = tiles_of(NBLK, NB)
        load_x(xbs[0], *tiles[0])
        yc = 0
        for ti, (b0, nb) in enumerate(tiles):
            W = nb * 128
            assert W == 256
            xt = xbs[ti % 2]
            hT = hTs[ti % 2]
            if ti + 1 < len(tiles):
                load_x(xbs[(ti + 1) % 2], *tiles[ti + 1])
            if ti == 0:
                prologue(xt, xn, hT, b0, nb, l, 1)
            tt("pool", ct1.ap[:], carry.ap[:, :, :, 0], dw0v, ALU.mult, [carry, dwc], [ct1])
            tt("pool", ct2.ap[:], carry.ap[:, :, :, 1], dw1v, ALU.mult, [carry, dwc], [ct2])
            tt("pool", cc.ap[:, :, :, 0], ct1.ap[:], ct2.ap[:], ALU.add, [ct1, ct2], [cc])
            tt("pool", cc.ap[:, :, :, 1], carry.ap[:, :, :, 1], dw0v, ALU.mult, [carry, dwc], [cc])
            tails = []

            def ffn_tail(j_, y_, s__, yr_, W=W):
                act(s__.ap[:, 0:W], y_.ap[:, 0, 0:W], AF.Silu, [yr_[0]], [s__])
                tt("pool", aT.ap[:, j_, 0:W], s__.ap[:, 0:W], y_.ap[:, 1, 0:W], ALU.mult, [s__, yr_[1]], [aTr[j_]])

            for j in range(FJ):
                pb = bank()
                for part in range(2):
                    off = part * FFN + j * 128
                    for k in range(KC):
                        mm(pb.ap[:, part * W:(part + 1) * W], w_in.ap[:, k, off:off + 128], hT.ap[:, k, 0:W], k == 0, k == KC - 1,
                           [wgrp[j], hT], [pb])
                y = ybufs[yc % NY]
                yr = yres[yc % NY]
                s_ = sbufs[yc % 3]
                yc += 1
                for part in range(2):
                    c = part * FJ + j
                    o_ = part * W
                    act(y.ap[:, part, 0:W], pb.ap[:, o_:o_ + W], AF.Identity, [pb, dwc, dwb], [yr[part]],
                        scale=dwc.ap[:, 2, c:c + 1], bias=dwb.ap[:, c:c + 1])
                act(carry.ap[:, j, :, :], pb.ap[:, :].rearrange("p (a w) -> p a w", a=2)[:, :, W - 2:W], AF.Copy, [pb], [carry])
                for tap, sh_ in ((1, 1), (0, 2)):
                    for part in range(2):
                        c = part * FJ + j
                        o_ = part * W
                        stt("dve", y.ap[:, part, sh_:W], pb.ap[:, o_:o_ + W - sh_], dwc.ap[:, tap, c:c + 1], y.ap[:, part, sh_:W],
                            ALU.mult, ALU.add, [pb, dwc, yr[part]], [yr[part]])
                tt("pool", y.ap[:, :, 0:2], y.ap[:, :, 0:2], cc.ap[:, j, :, :], ALU.add, [yr[0], yr[1], cc], [yr[0], yr[1]])
                tails.append((j, y, s_, yr))
                if len(tails) > 2:
                    ffn_tail(*tails.pop(0))
            while tails:
                ffn_tail(*tails.pop(0))
            out_proj_pipelined(ti, tiles, xbs, xn, hTs, l, 1,
                               lambda b, xt=xt, b0=b0: out_proj_block(xt, b, b0 + b, aT, FJ, w_out, None, aTr))
            store_x(xt, b0, nb, dst)
        state["src"] = dst

    def attn_pass(l, ja, dst):
        pass_begin()
        NB = 4
        WMAX = NB * 128
        wqk = A.alloc("wqk", [128, KC, 1280], BF16)
        wv = A.alloc("wv", [128, KC, 256], BF16)
        wo = A.alloc("wo", [128, KC, D], BF16)
        bqk = A.alloc("bqk", [128, 10], F32)
        bvb = A.alloc("bvb", [128, 256], F32)
        Ecur = A.alloc("Ecur", [128, 2048], F32)
        Eprev = A.alloc("Eprev", [128, 2048], F32)
        sk = A.alloc("sk", [128, 16], F32)
        sinkE = A.alloc("sinkE", [128, 16, 128], F32)
        brow = A.alloc("abo_rb", [128, D], BF16)
        mark = A.off
        gbc = load_gbc(l * 2 + 0)
        dma("pool", wqk.ap[:], wqk_d[ja], [], [wqk], "wld")
        dma("pool", wv.ap[:], wv_d[ja], [], [wv], "wld")
        dma("sp", bqk.ap[:], bqk_col_d[:, ja, :], [], [bqk], "ld")
        dma("sp", bvb.ap[:], bv_d[ja].partition_broadcast(128), [], [bvb], "ld")
        dma("sp", Ecur.ap[:], E_cur_d, [], [Ecur], "ld")
        dma("sp", Eprev.ap[:], E_prev_d, [], [Eprev], "ld")
        dma("sp", sk.ap[:], sinks_d[ja].partition_broadcast(128), [], [sk], "ld")
        act(sk.ap[:], sk.ap[:], AF.Exp, [sk], [sk])
        cp("dve", sinkE.ap[:], sk.ap[:, :].unsqueeze(2).to_broadcast([128, 16, 128]), [sk], [sinkE])
        load_w_scaled(wo, wo_d[ja], KC, gbc, "awo")
        bias_row_scaled(bo_d[ja], gbc, "abo", brow)
        end_setup(mark)
        xbs = [A.alloc("xb%d" % i, [128, NB, D], F32) for i in range(2)]
        xn = A.alloc("xn", [128, NB, D], BF16)
        hTs = [A.alloc("hT%d" % i, [128, KC, WMAX], BF16) for i in range(2)]
        qT = A.alloc("qT", [128, KC, WMAX], BF16)
        kT = A.alloc("kT", [128, 2, 128 + WMAX], BF16)
        V = A.alloc("V", [128, 1 + NB, 256], BF16)
        oT = A.alloc("oT", [128, KC, WMAX], BF16)
        ebufs = [A.alloc("e%d" % i, [128, 512], F32) for i in range(4)]
        pTs = [A.alloc("pT%d" % i, [128, 512], BF16) for i in range(6)]
        dens = [A.alloc("den%d" % i, [128, 512], F32) for i in range(2)]
        mset("pool", kT.ap[:], 0.0, [kT])
        mset("pool", V.ap[:], 0.0, [V])
        tiles = tiles_of(NBLK, NB)
        load_x(xbs[0], *tiles[0])
        ec = 0
        dcn = [0]
        for ti, (b0, nb) in enumerate(tiles):
            W = nb * 128
            xt = xbs[ti % 2]
            hT = hTs[ti % 2]
            if ti + 1 < len(tiles):
                load_x(xbs[(ti + 1) % 2], *tiles[ti + 1])
            if ti == 0:
                prologue(xt, xn, hT, b0, nb, l, 0)
            for c in range(10):
                pb = bank()
                for k in range(KC):
                    mm(pb.ap[:, 0:W], wqk.ap[:, k, c * 128:(c + 1) * 128], hT.ap[:, k, 0:W], k == 0, k == KC - 1,
                       [wqk, hT], [pb])
                if c < 8:
                    act(qT.ap[:, c, 0:W], pb.ap[:, 0:W], AF.Identity, [pb, bqk], [qT], bias=bqk.ap[:, c:c + 1])
                else:
                    act(kT.ap[:, c - 8, 128:128 + W], pb.ap[:, 0:W], AF.Identity, [pb, bqk], [kT], bias=bqk.ap[:, c:c + 1])
            for b in range(nb):
                pb = bank()
                for k in range(KC):
                    mm(pb.ap[:, 0:256], hT.ap[:, k, b * 128:(b + 1) * 128], wv.ap[:, k, :], k == 0, k == KC - 1,
                       [hT, wv], [pb])
                tt("dve", V.ap[:, 1 + b, :], pb.ap[:, 0:256], bvb.ap[:], ALU.add, [pb, bvb], [V])
            def attn_back(b, kv, pts):
                p_, e_ = kv // 2, kv % 2
                rows = slice(e_ * 64, (e_ + 1) * 64)
                pden = bank()
                po = bank()
                for i, (pt, v_ap) in enumerate(pts):
                    mm(pden.ap[:, :], ones_b.ap[:], pt.ap[:], i == 0, i == len(pts) - 1, [ones_b, pt], [pden])
                for i, (pt, v_ap) in enumerate(pts):
                    mm(po.ap[:, :], v_ap, pt.ap[:], i == 0, i == len(pts) - 1, [V, pt], [po])
                den = dens[dcn[0] % 2]
                dcn[0] += 1
                tt("dve", den.ap[rows, :], pden.ap[rows, :],
                   sinkE.ap[rows, kv * 4:(kv + 1) * 4, :].rearrange("p a b -> p (a b)"), ALU.add, [pden, sinkE], [den])
                P.op("dve", lambda e, den=den, rows=rows: e.reciprocal(out=den.ap[rows, :], in_=den.ap[rows, :]),
                     [den.r], [den.r])
                tt("dve", oT.ap[rows, p_ * 4:(p_ + 1) * 4, b * 128:(b + 1) * 128],
                   po.ap[rows, :].rearrange("p (a b) -> p a b", a=4),
                   den.ap[rows, :].rearrange("p (a b) -> p a b", a=4), ALU.mult, [po, den], [oT])

            pend = []
            for b in range(nb):
                blk = b0 + b
                has_prev = blk > 0
                for kv in range(4):
                    p_, e_ = kv // 2, kv % 2
                    rows = slice(e_ * 64, (e_ + 1) * 64)
                    q_ap = qT.ap[rows, p_ * 4:(p_ + 1) * 4, b * 128:(b + 1) * 128]
                    srcs = [("cur", kT.ap[rows, p_, 128 + b * 128:128 + (b + 1) * 128], Ecur, V.ap[:, 1 + b, p_ * 128:(p_ + 1) * 128])]
                    if has_prev:
                        srcs.append(("prev", kT.ap[rows, p_, b * 128:(b + 1) * 128], Eprev, V.ap[:, b, p_ * 128:(p_ + 1) * 128]))
                    pts = []
                    for (nm, k_ap, Etab, v_ap) in srcs:
                        pb = bank()
                        mm(pb.ap[:, :], k_ap, q_ap, True, True, [kT, qT], [pb])
                        eb = ebufs[ec % 4]
                        pt = pTs[ec % 6]
                        ec += 1
                        act(eb.ap[:], pb.ap[:, :], AF.Exp, [pb], [eb], scale=0.125)
                        tt("pool", pt.ap[:], eb.ap[:], Etab.ap[:, kv * 512:(kv + 1) * 512], ALU.mult, [eb, Etab], [pt])
                        pts.append((pt, v_ap))
                    pend.append((b, kv, pts))
                    if len(pend) > 1:
                        attn_back(*pend.pop(0))
            while pend:
                attn_back(*pend.pop(0))
            out_proj_pipelined(ti, tiles, xbs, xn, hTs, l, 0,
                               lambda b, xt=xt, b0=b0: out_proj_block(xt, b, b0 + b, oT, KC, wo, brow))
            cp("pool", kT.ap[:, :, 0:128], kT.ap[:, :, W:W + 128], [kT], [kT])
            cp("pool", V.ap[:, 0, :], V.ap[:, nb, :], [V], [V])
            store_x(xt, b0, nb, dst)
        state["src"] = dst

    def conv_pass(l, dst):
        pass_begin()
        NB = 2
        WMAX = NB * 128
        w_in = A.alloc("cw_in", [128, KC, 2048], BF16)
        w_out = A.alloc("cw_out", [128, KC, D], BF16)
        diag = A.alloc("cdiag", [128, KC * 31, 128], BF16)
        bin_c = A.alloc("cbin", [128, 16], F32)
        dwc = A.alloc("cdwc", [128, 31, KC], F32)
        dwb = A.alloc("cdwb", [128, KC], F32)
        lng = A.alloc("clng", [128, KC], F32)
        lnb = A.alloc("clnb", [128, KC], F32)
        brow = A.alloc("cbo_rb", [128, D], BF16)
        mark = A.off
        gbc = load_gbc(l * 2 + 0)
        dma("pool", w_in.ap[:], cw_in_d, [], [w_in], "wld")
        for (bf, d_) in ((bin_c, cb_in_col_d), (dwc, cdw_col_d), (dwb, cdwb_col_d), (lng, clng_col_d), (lnb, clnb_col_d)):
            dma("sp", bf.ap[:], d_, [], [bf], "ld")
        for c in range(KC):
            for t in range(31):
                ts("pool", diag.ap[:, c * 31 + t, :], ident_f.ap[:], dwc.ap[:, t, c:c + 1], None, ALU.mult, None,
                   [ident_f, dwc], [diag])
        load_w_scaled(w_out, cw_out_d, KC, gbc, "cwo")
        bias_row_scaled(cb_out_d, gbc, "cbo", brow)
        end_setup(mark)
        xbs = [A.alloc("xb%d" % i, [128, NB, D], F32) for i in range(2)]
        xn = A.alloc("xn", [128, NB, D], BF16)
        hTs = [A.alloc("hT%d" % i, [128, KC, WMAX], BF16) for i in range(2)]
        zb = A.alloc("zb", [128, KC, 30 + WMAX], BF16)
        yb = A.alloc("yb", [128, KC, WMAX], F32)
        ybr = [Res("yb%d" % i) for i in range(KC)]
        ybf = [A.alloc("ybf%d" % i, [128, WMAX], BF16) for i in range(3)]
        ysq = [A.alloc("ysq%d" % i, [128, WMAX], BF16) for i in range(3)]
        tmp8 = [A.alloc("tmp8_%d" % i, [128, WMAX], F32) for i in range(KC)]
        sg8 = [A.alloc("sg8_%d" % i, [128, WMAX], F32) for i in range(KC)]
        sg = [A.alloc("sg%d" % i, [128, WMAX], F32) for i in range(2)]
        mean = A.alloc("mean", [128, WMAX], F32)
        rs = A.alloc("rs", [128, WMAX], F32)
        tmp = [A.alloc("tmp%d" % i, [128, WMAX], F32) for i in range(2)]
        sT = A.alloc("sT", [128, KC, WMAX], BF16)
        sTr = [Res("sT%d" % i) for i in range(KC)]
        mset("pool", zb.ap[:], 0.0, [zb])
        tiles = tiles_of(NBLK, NB)
        load_x(xbs[0], *tiles[0])
        for ti, (b0, nb) in enumerate(tiles):
            W = nb * 128
            xt = xbs[ti % 2]
            hT = hTs[ti % 2]
            if ti + 1 < len(tiles):
                load_x(xbs[(ti + 1) % 2], *tiles[ti + 1])
            if ti == 0:
                prologue(xt, xn, hT, b0, nb, l, 0)
            for c in range(KC):
                pa = bank()
                pg = bank()
                for k in range(KC):
                    mm(pa.ap[:, 0:W], w_in.ap[:, k, c * 128:(c + 1) * 128], hT.ap[:, k, 0:W], k == 0, k == KC - 1, [w_in, hT], [pa])
                for k in range(KC):
                    mm(pg.ap[:, 0:W], w_in.ap[:, k, D + c * 128:D + (c + 1) * 128], hT.ap[:, k, 0:W], k == 0, k == KC - 1,
                       [w_in, hT], [pg])
                s_ = sg[c % 2]
                act(s_.ap[:, 0:W], pg.ap[:, 0:W], AF.Sigmoid, [pg, bin_c], [s_], bias=bin_c.ap[:, 8 + c:9 + c])
                stt("dve", zb.ap[:, c, 30:30 + W], pa.ap[:, 0:W], bin_c.ap[:, c:c + 1], s_.ap[:, 0:W], ALU.add, ALU.mult,
                    [pa, bin_c, s_], [zb])
            pmean = bank()
            pmsq = bank()
            bank_excl.extend([pmean, pmsq])
            pend_stats = []

            def conv_stats(c_, yb16_, ys16_, W=W):
                mm(pmean.ap[:, 0:W], onesN_b.ap[:], yb16_.ap[:, 0:W], c_ == 0, c_ == KC - 1, [onesN_b, yb16_], [pmean])
                mm(pmsq.ap[:, 0:W], onesN_b.ap[:], ys16_.ap[:, 0:W], c_ == 0, c_ == KC - 1, [onesN_b, ys16_], [pmsq])

            for c in range(KC):
                py = bank()
                for t in range(31):
                    mm(py.ap[:, 0:W], diag.ap[:, c * 31 + t, :], zb.ap[:, c, t:t + W], t == 0, t == 30, [diag, zb], [py])
                act(yb.ap[:, c, 0:W], py.ap[:, 0:W], AF.Identity, [py, dwb], [ybr[c]], bias=dwb.ap[:, c:c + 1])
                yb16 = ybf[c % 3]
                ys16 = ysq[c % 3]
                act(ys16.ap[:, 0:W], py.ap[:, 0:W], AF.Square, [py, dwb], [ys16], bias=dwb.ap[:, c:c + 1])
                cp("pool", yb16.ap[:, 0:W], yb.ap[:, c, 0:W], [ybr[c]], [yb16])
                pend_stats.append((c, yb16, ys16))
                if len(pend_stats) > 1:
                    conv_stats(*pend_stats.pop(0))
            while pend_stats:
                conv_stats(*pend_stats.pop(0))
            del bank_excl[:]
            cp("dve", mean.ap[:, 0:W], pmean.ap[:, 0:W], [pmean], [mean])
            tt("dve", rs.ap[:, 0:W], mean.ap[:, 0:W], mean.ap[:, 0:W], ALU.mult, [mean], [rs])
            tt("dve", rs.ap[:, 0:W], pmsq.ap[:, 0:W], rs.ap[:, 0:W], ALU.subtract, [pmsq, rs], [rs])
            ts("dve", rs.ap[:, 0:W], rs.ap[:, 0:W], EPS, None, ALU.add, None, [rs], [rs])
            act(rs.ap[:, 0:W], rs.ap[:, 0:W], AF.Sqrt, [rs], [rs])
            P.op("dve", lambda e, W=W: e.reciprocal(out=rs.ap[:, 0:W], in_=rs.ap[:, 0:W]), [rs.r], [rs.r])
            for c in range(KC):
                tt("dve", tmp8[c].ap[:, 0:W], yb.ap[:, c, 0:W], mean.ap[:, 0:W], ALU.subtract, [ybr[c], mean], [tmp8[c]])
            for c in range(KC):
                tt("pool", tmp8[c].ap[:, 0:W], tmp8[c].ap[:, 0:W], rs.ap[:, 0:W], ALU.mult, [tmp8[c], rs], [tmp8[c]])
            for c in range(KC):
                act(sg8[c].ap[:, 0:W], tmp8[c].ap[:, 0:W], AF.Sigmoid, [tmp8[c], lng, lnb], [sg8[c]],
                    scale=lng.ap[:, c:c + 1], bias=lnb.ap[:, c:c + 1])
            for c in range(KC):
                ts("dve", tmp8[c].ap[:, 0:W], tmp8[c].ap[:, 0:W], lng.ap[:, c:c + 1], lnb.ap[:, c:c + 1], ALU.mult, ALU.add,
                   [tmp8[c], lng, lnb], [tmp8[c]])
            for c in range(KC):
                tt("pool", sT.ap[:, c, 0:W], tmp8[c].ap[:, 0:W], sg8[c].ap[:, 0:W], ALU.mult, [tmp8[c], sg8[c]], [sTr[c]])
            out_proj_pipelined(ti, tiles, xbs, xn, hTs, l, 0,
                               lambda b, xt=xt, b0=b0: out_proj_block(xt, b, b0 + b, sT, KC, w_out, brow, sTr))
            cp("pool", zb.ap[:, :, 0:30], zb.ap[:, :, W:W + 30], [zb], [zb])
            store_x(xt, b0, nb, dst)
        state["src"] = dst

    def sgu_pass(l, dst):
        pass_begin()
        NB = 2
        WMAX = NB * 128
        w_in = A.alloc("sw_in", [128, KC, 4096], BF16)
        w_out = A.alloc("sw_out", [128, 16, D], BF16)
        buc = A.alloc("sbuc", [128, 16], F32)
        bvr = A.alloc("sbvr", [128, 2048], BF16)
        lngb = A.alloc("slngb", [128, 2048], F32)
        lnbb = A.alloc("slnbb", [128, 2048], F32)
        wmT = A.alloc("swmT", [128, 8, 128], BF16)
        bsr = A.alloc("sbsr", [128, 1024], BF16)
        brow = A.alloc("sbo_rb", [128, D], BF16)
        mark = A.off
        wm32 = A.alloc("swm32", [128, 8, 128], F32)
        tri = A.alloc("stri", [128, 128], F32)
        gbc = load_gbc(l * 2 + 0)
        dma("pool", w_in.ap[:], sw_in_d, [], [w_in], "wld")
        dma("sp", buc.ap[:], sb_in_ucol_d, [], [buc], "ld")
        dma("pool", bvr.ap[0:1, :], sb_in_v_d, [], [bvr], "wld")
        dma("sp", lngb.ap[:], slng_d.partition_broadcast(128), [], [lngb], "ld")
        dma("sp", lnbb.ap[:], slnb_d.partition_broadcast(128), [], [lnbb], "ld")
        dma("sp", wm32.ap[:], swsT_d, [], [wm32], "ld")
        dma("sp", tri.ap[:], tri_d, [], [tri], "ld")
        for g in range(8):
            tt("dve", wmT.ap[:, g, :], wm32.ap[:, g, :], tri.ap[:], ALU.mult, [wm32, tri], [wmT])
        dma("pool", bsr.ap[0:1, :], sbs_d, [], [bsr], "wld")
        load_w_scaled(w_out, sw_out_d, 16, gbc, "swo")
        bias_row_scaled(sb_out_d, gbc, "sbo", brow)
        end_setup(mark)
        xbs = [A.alloc("xb%d" % i, [128, NB, D], F32) for i in range(2)]
        xn = A.alloc("xn", [128, NB, D], BF16)
        hTs = [A.alloc("hT%d" % i, [128, KC, WMAX], BF16) for i in range(2)]
        uT = A.alloc("uT", [128, 16, WMAX], F32)
        vraw = A.alloc("vraw", [128, 2048], F32)
        vrq = [Res("vraw%d" % i) for i in range(4)]
        vln = A.alloc("vln", [128, 2048], BF16)
        st6 = A.alloc("st6", [128, 4, 6], F32)
        mv = A.alloc("mv", [128, 2], F32)
        rsd = A.alloc("rsd", [128, 1], F32)
        mT = A.alloc("mT", [128, 16, WMAX], BF16)
        tiles = tiles_of(NBLK, NB)
        load_x(xbs[0], *tiles[0])
        for ti, (b0, nb) in enumerate(tiles):
            W = nb * 128
            xt = xbs[ti % 2]
            hT = hTs[ti % 2]
            if ti + 1 < len(tiles):
                load_x(xbs[(ti + 1) % 2], *tiles[ti + 1])
            if ti == 0:
                prologue(xt, xn, hT, b0, nb, l, 0)
            def u_proj(W=W, hT=hT):
                for c in range(16):
                    pb = bank()
                    for k in range(KC):
                        mm(pb.ap[:, 0:W], w_in.ap[:, k, c * 128:(c + 1) * 128], hT.ap[:, k, 0:W], k == 0, k == KC - 1, [w_in, hT], [pb])
                    act(uT.ap[:, c, 0:W], pb.ap[:, 0:W], AF.Gelu, [pb, buc], [uT], bias=buc.ap[:, c:c + 1])

            def v_proj_ln(b, hT=hT):
                pbs = [bank() for _ in range(4)]
                for q in range(4):
                    mm(pbs[q].ap[:, :], ones_b.ap[0:1, :], bvr.ap[0:1, q * 512:(q + 1) * 512], True, False, [ones_b, bvr], [pbs[q]])
                for k in range(KC):
                    for q in range(4):
                        mm(pbs[q].ap[:, :], hT.ap[:, k, b * 128:(b + 1) * 128], w_in.ap[:, k, 2048 + q * 512:2048 + (q + 1) * 512],
                           False, k == KC - 1, [hT, w_in], [pbs[q]])
                for q in range(4):
                    act(vraw.ap[:, q * 512:(q + 1) * 512], pbs[q].ap[:, :], AF.Gelu, [pbs[q]], [vrq[q]])
                for q in range(4):
                    P.op("dve", lambda e, q=q: e.bn_stats(out=st6.ap[:, q, :], in_=vraw.ap[:, q * 512:(q + 1) * 512]),
                         [vrq[q]], [st6.r])
                P.op("dve", lambda e: e.bn_aggr(out=mv.ap[:, :], in_=st6.ap[:, :, :]), [st6.r], [mv.r])
                ts("dve", rsd.ap[:], mv.ap[:, 1:2], EPS, None, ALU.add, None, [mv], [rsd])
                act(rsd.ap[:], rsd.ap[:], AF.Sqrt, [rsd], [rsd])
                P.op("dve", lambda e: e.reciprocal(out=rsd.ap[:], in_=rsd.ap[:]), [rsd.r], [rsd.r])
                ts("dve", vraw.ap[:], vraw.ap[:], mv.ap[:, 0:1], rsd.ap[:, 0:1], ALU.subtract, ALU.mult, vrq + [mv, rsd], vrq)
                tt("pool", vraw.ap[:], vraw.ap[:], lngb.ap[:], ALU.mult, vrq + [lngb], vrq)
                tt("pool", vln.ap[:], vraw.ap[:], lnbb.ap[:], ALU.add, vrq + [lnbb], [vln])

            def gating(b):
                for q in range(4):
                    pb = bank()
                    for cc in range(4):
                        fc = q * 4 + cc
                        g = fc // 2
                        mm(pb.ap[:, cc * 128:(cc + 1) * 128], ones_b.ap[0:1, :], bsr.ap[0:1, g * 128:(g + 1) * 128], True, False,
                           [ones_b, bsr], [pb])
                        mm(pb.ap[:, cc * 128:(cc + 1) * 128], vln.ap[:, fc * 128:(fc + 1) * 128], wmT.ap[:, g, :], False, True,
                           [vln, wmT], [pb])
                    tt("dve", mT.ap[:, q * 4:(q + 1) * 4, b * 128:(b + 1) * 128], pb.ap[:, :].rearrange("p (a b) -> p a b", a=4),
                       uT.ap[:, q * 4:(q + 1) * 4, b * 128:(b + 1) * 128], ALU.mult, [pb, uT], [mT])

            nxt = tiles[ti + 1] if ti + 1 < len(tiles) else None
            v_proj_ln(0)
            u_proj()
            gating(0)
            for b in range(1, nb):
                v_proj_ln(b)
                if nxt is not None and b == 1:
                    prologue_a(xbs[(ti + 1) % 2], xn, nxt[0], nxt[1])
                out_proj_block(xt, b - 1, b0 + b - 1, mT, 16, w_out, brow)
                gating(b)
            if nxt is not None:
                if nb == 1:
                    prologue_a(xbs[(ti + 1) % 2], xn, nxt[0], nxt[1])
                prologue_b(xn, hTs[(ti + 1) % 2], nxt[1], l, 0)
            out_proj_block(xt, nb - 1, b0 + nb - 1, mT, 16, w_out, brow)
            store_x(xt, b0, nb, dst)
        state["src"] = dst

    def final_pass():
        pass_begin()
        NB = 2
        fg = A.alloc("fg", [128, D], F32)
        dma("sp", fg.ap[:], final_g.partition_broadcast(128), [], [fg], "ld")
        xbs = [A.alloc("xb%d" % i, [128, NB, D], F32) for i in range(2)]
        tiles = tiles_of(NBLK, NB)
        load_x(xbs[0], *tiles[0])
        for ti, (b0, nb) in enumerate(tiles):
            xt = xbs[ti % 2]
            if ti + 1 < len(tiles):
                load_x(xbs[(ti + 1) % 2], *tiles[ti + 1])
            for b in range(nb):
                act(xt.ap[:, b, :], xt.ap[:, b, :], AF.Copy, [xt, rstd], [xt], scale=rstd.ap[:, b0 + b:b0 + b + 1])
                tt("dve" if b % 2 == 0 else "pool", xt.ap[:, b, :], xt.ap[:, b, :], fg.ap[:], ALU.mult, [xt, fg], [xt])
            store_x(xt, b0, nb, out_d)

    for pname in passes:
        kind, l = pname[0], int(pname[1])
        if kind == "a":
            attn_pass(l, l // 3, xs_d)
        elif kind == "c":
            conv_pass(l, xs_d)
        elif kind == "s":
            sgu_pass(l, xs_d)
        elif kind == "f":
            ffn_pass(l, xs_d)
    if debug_out:
        P.barrier()
        A.reset()
        xb = A.alloc("xdbg", [128, 2, D], F32)
        for (b0, nb) in tiles_of(NBLK, 2):
            load_x(xb, b0, nb)
            store_x(xb, b0, nb, out_d)
    else:
        final_pass()
    _LAST_PROG[0] = P
    P.emit()
    es.close()
    return nc


def _cols(v, n):
    return np.ascontiguousarray(np.asarray(v, np.float32).reshape(n, 128).T)


def _wk(w):
    w = np.asarray(w, np.float32)
    K, N = w.shape
    return np.ascontiguousarray(w.reshape(K // 128, 128, N).transpose(1, 0, 2))


def _alibi_tables():
    slopes = (2.0 ** (-8.0 * np.arange(1, 17, dtype=np.float64) / 16)).reshape(4, 4)
    s = np.arange(128)[:, None]
    q = np.arange(128)[None, :]
    dist_cur = (q - s).astype(np.float64)
    dist_prev = (q + 128 - s).astype(np.float64)
    Ec = np.zeros((128, 4, 4, 128), np.float64)
    Ep = np.zeros((128, 4, 4, 128), np.float64)
    for kv in range(4):
        for g in range(4):
            Ec[:, kv, g, :] = np.where((dist_cur >= 0) & (dist_cur < 128), np.exp(-slopes[kv, g] * dist_cur), 0.0)
            Ep[:, kv, g, :] = np.where((dist_prev >= 0) & (dist_prev < 128), np.exp(-slopes[kv, g] * dist_prev), 0.0)
    return Ec.reshape(128, 2048).astype(np.float32), Ep.reshape(128, 2048).astype(np.float32)


def prep_shared(inp):
    f = lambda a: np.asarray(a, np.float32)
    sh = {}
    aw = f(inp["ada_w"]).reshape(NL, KC, 128, 6, D)
    sh["adaw"] = np.ascontiguousarray(aw.transpose(0, 3, 2, 1, 4))
    ab = f(inp["ada_b"]).reshape(NL, 6, KC, 128)
    sh["adab_col"] = np.ascontiguousarray(ab.transpose(3, 0, 1, 2))
    sh["adab_row"] = np.ascontiguousarray(f(inp["ada_b"]).reshape(NL, 6, D))
    ng = np.stack([f(inp["norm1_g"]), f(inp["norm2_g"])], 1).reshape(NL, 2, KC, 128)
    sh["ng_col"] = np.ascontiguousarray(ng.transpose(3, 0, 1, 2))
    sh["final_g"] = f(inp["final_g"]).reshape(1, D)
    sh["ident"] = np.eye(128, dtype=np.float32)
    sidx = np.arange(128)[:, None]
    tidx = np.arange(128)[None, :]
    sh["tri"] = (tidx >= sidx).astype(np.float32)
    sh["E_cur"], sh["E_prev"] = _alibi_tables()
    heads = []
    for p in range(2):
        for g in range(4):
            heads += [4 * (2 * p) + g, 4 * (2 * p + 1) + g]
    qcols = np.concatenate([np.arange(h * 64, (h + 1) * 64) for h in heads])
    wqkv = f(inp["attn_wqkv"])
    bqkv = f(inp["attn_bqkv"])
    wqk = np.concatenate([wqkv[:, :, qcols], wqkv[:, :, 1024:1280]], axis=2)
    sh["wqk"] = np.stack([_wk(wqk[j]) for j in range(2)])
    sh["wv"] = np.stack([_wk(wqkv[j][:, 1280:1536]) for j in range(2)])
    bqk = np.concatenate([bqkv[:, qcols], bqkv[:, 1024:1280]], axis=1)
    sh["bqk_col"] = np.ascontiguousarray(np.stack([_cols(bqk[j], 10) for j in range(2)], 1))
    sh["bv"] = np.ascontiguousarray(bqkv[:, 1280:1536].reshape(2, 1, 256))
    wo = f(inp["attn_wo"])
    sh["wo"] = np.stack([_wk(wo[j][qcols, :]) for j in range(2)])
    sh["bo"] = f(inp["attn_bo"]).reshape(2, 1, D)
    sh["sinks"] = f(inp["attn_sinks"]).reshape(2, 1, 16)
    sh["cw_in"] = _wk(f(inp["conv_w_in"])[0])
    sh["cb_in_col"] = _cols(f(inp["conv_b_in"])[0], 16)
    cdw = f(inp["conv_dw"])[0].reshape(31, KC, 128)
    sh["cdw_col"] = np.ascontiguousarray(cdw.transpose(2, 0, 1))
    sh["cdwb_col"] = _cols(f(inp["conv_dw_b"])[0], KC)
    sh["clng_col"] = _cols(f(inp["conv_ln_g"])[0], KC)
    sh["clnb_col"] = _cols(f(inp["conv_ln_b"])[0], KC)
    sh["cw_out"] = _wk(f(inp["conv_w_out"])[0])
    sh["cb_out"] = f(inp["conv_b_out"]).reshape(1, D)
    sh["sw_in"] = _wk(f(inp["sgu_w_in"])[0])
    sbin = f(inp["sgu_b_in"])[0]
    sh["sb_in_ucol"] = _cols(sbin[:2048], 16)
    sh["sb_in_v"] = np.ascontiguousarray(sbin[2048:].reshape(1, 2048))
    sh["slng"] = f(inp["sgu_ln_g"]).reshape(1, 2048)
    sh["slnb"] = f(inp["sgu_ln_b"]).reshape(1, 2048)
    sh["swsT"] = np.ascontiguousarray(f(inp["sgu_ws"])[0].transpose(2, 0, 1))
    sh["sbs"] = np.ascontiguousarray(f(inp["sgu_bs"])[0].reshape(1, 1024))
    sh["sw_out"] = _wk(f(inp["sgu_w_out"])[0])
    sh["sb_out"] = f(inp["sgu_b_out"]).reshape(1, D)
    fwi = f(inp["ffn_w_in"])
    sh["fw_in"] = np.stack([_wk(fwi[l]) for l in range(NL)])
    fdw = f(inp["ffn_dw"]).reshape(NL, 3, 44, 128)
    sh["fdw_col"] = np.ascontiguousarray(fdw.transpose(3, 0, 1, 2))
    fdwb = f(inp["ffn_dw_b"]).reshape(NL, 44, 128)
    sh["fdwb_col"] = np.ascontiguousarray(fdwb.transpose(2, 0, 1))
    fwo = f(inp["ffn_w_out"])
    sh["fw_out"] = np.stack([_wk(fwo[l]) for l in range(NL)])
    return sh


_NC_CACHE = {}
_LAST_PROG = [None]


def kernel(**inputs):
    x = np.asarray(inputs["x"], np.float32)
    c = np.asarray(inputs["c"], np.float32)
    B = x.shape[0]
    shared = prep_shared(inputs)
    NBLK = NBLK_FULL
    T = NBLK * 128
    if "nc" not in _NC_CACHE:
        _NC_CACHE["nc"] = build_program(NBLK)
    nc = _NC_CACHE["nc"]
    in_maps = []
    for core in range(8):
        b, half = core // 2, core % 2
        t0 = 0 if half == 0 else SEQ - T
        m = dict(shared)
        m["x"] = np.ascontiguousarray(x[b, t0:t0 + T, :])
        m["ccol"] = _cols(c[b], KC)
        in_maps.append(m)
    res = run_bass_kernel_spmd(nc, in_maps, core_ids=list(range(8)))
    out = np.empty((B, SEQ, D), np.float32)
    for core in range(8):
        b, half = core // 2, core % 2
        o = np.asarray(res.results[core]["out"]).reshape(T, D)
        if half == 0:
            out[b, 0:T] = o
        else:
            out[b, T:SEQ] = o[2 * T - SEQ:]
    return out
```
